# Optimizing a Trainium2 kernel written in Bass

```python
import jax, jax.numpy as jnp
from jax import lax
import numpy as np

D_MODEL = 2048
BATCH = 4
SEQ = 4096
DEPTH = 4

CTX_LEN = 256
GRID_W = 64
N_MIXERS = 2
N_HEADS = 16
HEAD_DIM = D_MODEL // N_HEADS
WIN_ROWS = 8
WIN_COLS = 16
Q_COLS = 16
K_COLS = Q_COLS + WIN_COLS
POOL_WINDOWS = (2, 4, 8, 16)
N_POOL_GROUPS = len(POOL_WINDOWS)
POOL_GROUP = D_MODEL // N_POOL_GROUPS
D_FF = -(-8 * D_MODEL // (3 * 256)) * 256
N_ADA = 6
LN_EPS = 1e-5
DN_ALPHA = (2 * DEPTH) ** 0.25
DN_BETA = (8 * DEPTH) ** -0.25
N_NA_LAYERS = len([i for i in range(DEPTH) if i % N_MIXERS == 0])
N_POOL_LAYERS = DEPTH - N_NA_LAYERS

kernel_name = "hybrid_natten_pool_deepnorm_dit"


def layer_norm(x, g, b):
    xf = x.astype(jnp.float32)
    mu = xf.mean(-1, keepdims=True)
    var = jnp.square(xf - mu).mean(-1, keepdims=True)
    y = (xf - mu) * lax.rsqrt(var + LN_EPS) * g.astype(jnp.float32) + b.astype(jnp.float32)
    return y.astype(x.dtype)


def ada_params(cond, w, b):
    return jnp.split(jax.nn.silu(cond) @ w + b, N_ADA, axis=-1)


def modulate(x, shift, scale):
    return x * (1 + scale) + shift


def split_heads(t):
    return t.reshape(*t.shape[:-1], N_HEADS, HEAD_DIM)


def neighbourhood_attention(q, k, v, k_ctx, v_ctx, rpb):
    B, L = q.shape[0], q.shape[1]
    rows = L // GRID_W
    kr = min(WIN_ROWS, rows)
    n_cb = GRID_W // Q_COLS
    scale = HEAD_DIM ** -0.5
    q_col = np.arange(GRID_W).reshape(n_cb, Q_COLS)
    win_start = np.clip(q_col - WIN_COLS // 2, 0, GRID_W - WIN_COLS)
    band_start = np.clip(np.arange(n_cb) * Q_COLS - WIN_COLS // 2, 0, GRID_W - K_COLS)
    key_col_np = band_start[:, None] + np.arange(K_COLS)
    kc = key_col_np[:, None, :]
    col_ok = (kc >= win_start[:, :, None]) & (kc < win_start[:, :, None] + WIN_COLS)
    col_ok = jnp.asarray(col_ok)[:, :, None, :]
    dc_idx = jnp.asarray(np.clip(kc - q_col[:, :, None] + WIN_COLS - 1, 0, 2 * WIN_COLS - 2))[:, :, None, :]
    key_col = jnp.asarray(key_col_np)
    qg = q.reshape(B, rows, GRID_W, N_HEADS, HEAD_DIM)
    kg = k.reshape(B, rows, GRID_W, N_HEADS, HEAD_DIM)
    vg = v.reshape(B, rows, GRID_W, N_HEADS, HEAD_DIM)
    n_win = kr * K_COLS

    def row_block(r):
        r0 = jnp.clip(r - kr // 2, 0, rows - kr)
        q_blk = lax.dynamic_index_in_dim(qg, r, axis=1, keepdims=False).reshape(B, n_cb, Q_COLS, N_HEADS, HEAD_DIM)
        k_blk = lax.dynamic_slice_in_dim(kg, r0, kr, axis=1)[:, :, key_col]
        v_blk = lax.dynamic_slice_in_dim(vg, r0, kr, axis=1)[:, :, key_col]
        dr_idx = (r0 + jnp.arange(kr) - r + WIN_ROWS - 1)[None, None, :, None]
        bias = rpb[:, dr_idx, dc_idx]
        s_win = jnp.einsum('bjqhd,brjkhd->bhjqrk', q_blk, k_blk).astype(jnp.float32) * scale
        s_win = jnp.where(col_ok, s_win + bias.astype(jnp.float32), -jnp.inf)
        s_win = s_win.reshape(B, N_HEADS, n_cb, Q_COLS, n_win)
        s_ctx = jnp.einsum('bjqhd,bchd->bhjqc', q_blk, k_ctx).astype(jnp.float32) * scale
        p = jax.nn.softmax(jnp.concatenate([s_win, s_ctx], axis=-1), axis=-1).astype(v.dtype)
        p_win = p[..., :n_win].reshape(B, N_HEADS, n_cb, Q_COLS, kr, K_COLS)
        p_ctx = p[..., n_win:]
        o = (jnp.einsum('bhjqrk,brjkhd->bjqhd', p_win, v_blk)
             + jnp.einsum('bhjqc,bchd->bjqhd', p_ctx, v_ctx))
        return o.reshape(B, GRID_W, D_MODEL)

    o = lax.map(row_block, jnp.arange(rows))
    return jnp.moveaxis(o, 0, 1).reshape(B, L, D_MODEL)


def context_attention(q_ctx, k_ctx, v_ctx):
    B, C = q_ctx.shape[0], q_ctx.shape[1]
    s = jnp.einsum('bqhd,bkhd->bhqk', q_ctx, k_ctx).astype(jnp.float32) * HEAD_DIM ** -0.5
    p = jax.nn.softmax(s, axis=-1).astype(v_ctx.dtype)
    return jnp.einsum('bhqk,bkhd->bqhd', p, v_ctx).reshape(B, C, D_MODEL)


def multiscale_pool(h, w_pool, scale):
    B, L, _ = h.shape
    hf = h.astype(jnp.float32)
    csum = jnp.concatenate([jnp.zeros_like(hf[:, :1]), lax.cumsum(hf, axis=1)], axis=1)
    t = np.arange(L)
    groups = []
    for g, w in enumerate(POOL_WINDOWS):
        lo = np.clip(t - w // 2, 0, L)
        hi = np.clip(t - w // 2 + w, 0, L)
        sl = slice(g * POOL_GROUP, (g + 1) * POOL_GROUP)
        cg = csum[..., sl]
        mean = (cg[:, hi] - cg[:, lo]) / jnp.asarray(hi - lo, dtype=jnp.float32)[:, None]
        groups.append(mean - hf[..., sl])
    pooled = jnp.stack(groups, axis=2).astype(h.dtype)
    y = jnp.einsum('blgc,gcd->blgd', pooled, w_pool).reshape(B, L, D_MODEL)
    return y * scale


def swiglu(h, w_in, w_out):
    gate, up = jnp.split(h @ w_in, 2, axis=-1)
    return (jax.nn.silu(gate) * up) @ w_out


def setup_inputs(seed: int = 0) -> dict:
    key = jax.random.key(seed)
    ks = jax.random.split(key, 20)
    f32 = jnp.float32

    def nrm(k, shape, s):
        return jax.random.normal(k, shape, f32) * s

    v_scale = jnp.concatenate([jnp.ones((2 * D_MODEL,), f32), jnp.full((D_MODEL,), DN_BETA, f32)])
    return {
        "x": nrm(ks[0], (BATCH, SEQ, D_MODEL), 1.0),
        "c": nrm(ks[1], (BATCH, D_MODEL), 1.0),
        "ctx": nrm(ks[2], (BATCH, CTX_LEN, D_MODEL), 1.0),
        "c_ctx": nrm(ks[3], (D_MODEL,), 1.0),
        "ada_w": nrm(ks[4], (DEPTH, D_MODEL, N_ADA * D_MODEL), 0.5 * D_MODEL ** -0.5),
        "ada_b": nrm(ks[5], (DEPTH, N_ADA * D_MODEL), 0.02),
        "ln_mix_g": 1.0 + nrm(ks[6], (DEPTH, D_MODEL), 0.02),
        "ln_mix_b": nrm(ks[7], (DEPTH, D_MODEL), 0.02),
        "ln_ffn_g": 1.0 + nrm(ks[8], (DEPTH, D_MODEL), 0.02),
        "ln_ffn_b": nrm(ks[9], (DEPTH, D_MODEL), 0.02),
        "na_w_qkv": nrm(ks[10], (N_NA_LAYERS, D_MODEL, 3 * D_MODEL), D_MODEL ** -0.5) * v_scale,
        "na_w_o": nrm(ks[11], (N_NA_LAYERS, D_MODEL, D_MODEL), DN_BETA * D_MODEL ** -0.5),
        "na_rpb": nrm(ks[12], (N_NA_LAYERS, N_HEADS, 2 * WIN_ROWS - 1, 2 * WIN_COLS - 1), 0.1),
        "pool_w": nrm(ks[13], (N_POOL_LAYERS, N_POOL_GROUPS, POOL_GROUP, POOL_GROUP), DN_BETA * POOL_GROUP ** -0.5),
        "pool_scale": 1.0 + nrm(ks[14], (N_POOL_LAYERS, D_MODEL), 0.05),
        "ffn_w_in": nrm(ks[15], (DEPTH, D_MODEL, 2 * D_FF), D_MODEL ** -0.5),
        "ffn_w_out": nrm(ks[16], (DEPTH, D_FF, D_MODEL), DN_BETA * D_FF ** -0.5),
    }


def reference(x, c, ctx, c_ctx, ada_w, ada_b, ln_mix_g, ln_mix_b, ln_ffn_g, ln_ffn_b,
              na_w_qkv, na_w_o, na_rpb, pool_w, pool_scale, ffn_w_in, ffn_w_out):
    last_na = max(i for i in range(DEPTH) if i % N_MIXERS == 0)
    xc = ctx
    for i in range(DEPTH):
        use_na = i % N_MIXERS == 0
        j = i // N_MIXERS
        update_ctx = i < last_na
        sh1, sc1, g1, sh2, sc2, g2 = [m[:, None, :] for m in ada_params(c, ada_w[i], ada_b[i])]
        h = modulate(x, sh1, sc1)
        if update_ctx or use_na:
            csh1, csc1, cg1, csh2, csc2, cg2 = ada_params(c_ctx, ada_w[i], ada_b[i])
            hc = modulate(xc, csh1, csc1)
        if use_na:
            w_qkv = na_w_qkv[j]
            q, k, v = [split_heads(t) for t in jnp.split(h @ w_qkv, 3, axis=-1)]
            k_c, v_c = [split_heads(t) for t in jnp.split(hc @ w_qkv[:, D_MODEL:], 2, axis=-1)]
            y = neighbourhood_attention(q, k, v, k_c, v_c, na_rpb[j]) @ na_w_o[j]
            if update_ctx:
                q_c = split_heads(hc @ w_qkv[:, :D_MODEL])
                yc = context_attention(q_c, k_c, v_c) @ na_w_o[j]
        else:
            y = multiscale_pool(h, pool_w[j], pool_scale[j])
            if update_ctx:
                yc = multiscale_pool(hc, pool_w[j], pool_scale[j])
        x = layer_norm(DN_ALPHA * x + g1 * y, ln_mix_g[i], ln_mix_b[i])
        x = layer_norm(DN_ALPHA * x + g2 * swiglu(modulate(x, sh2, sc2), ffn_w_in[i], ffn_w_out[i]),
                       ln_ffn_g[i], ln_ffn_b[i])
        if update_ctx:
            xc = layer_norm(DN_ALPHA * xc + cg1 * yc, ln_mix_g[i], ln_mix_b[i])
            xc = layer_norm(DN_ALPHA * xc + cg2 * swiglu(modulate(xc, csh2, csc2), ffn_w_in[i], ffn_w_out[i]),
                            ln_ffn_g[i], ln_ffn_b[i])
    return x
```

```python
import contextlib
import numpy as np
import ml_dtypes
import concourse.bass as bass
import concourse.mybir as mybir
from concourse.bass_utils import run_bass_kernel_spmd

F32 = mybir.dt.float32
BF16 = mybir.dt.bfloat16
AF = mybir.ActivationFunctionType
ALU = mybir.AluOpType

D = 2048
KC = 16
NH = 16
DFF = 5632
FC = 44
CTX = 256
GW = 64
L = 4096
FR = 50
T = FR * GW
OWN0 = 10
DEPTH = 4
ALPHA = (2 * DEPTH) ** 0.25
EPS2 = 1e-5 / ALPHA ** 2
NEG = -30000.0
NE = 11
SCALE = 128 ** -0.5

SAME_ENGINE_SYNC = True
SEM_EPOCH = 30000


class Buf:
    def __init__(self, fw, name, ap=None):
        self.name = name
        self.ap = ap
        self.lw = None
        self.rd = {}
        self.dsem = None
        self.dcnt = 0
        self.fw = fw

    def get_dsem(self):
        if self.dsem is None:
            self.dsem = self.fw.new_sem("d_" + self.name)
            self.fw.dma_bufs.append(self)
        return self.dsem


class Eng:
    def __init__(self, fw, name, h, is_pe=False):
        self.fw = fw
        self.name = name
        self.h = h
        self.is_pe = is_pe
        self.sem = fw.new_sem("e_" + name)
        self.cnt = 0
        self.seen = {}
        self.nsem = 1
        self.prev = None

    def bump(self):
        if self.cnt >= SEM_EPOCH:
            self.prev = (self.sem, self.cnt)
            self.sem = self.fw.new_sem("e_%s%d" % (self.name, self.nsem))
            self.nsem += 1
            self.cnt = 0


class FW:
    def __init__(self, nc):
        self.nc = nc
        self.stack = contextlib.ExitStack()
        self.dma_bufs = []
        self.nsem = 0
        self.nbuf = 0
        self.pe = Eng(self, "pe", nc.tensor, is_pe=True)
        self.act = Eng(self, "act", nc.scalar)
        self.dve = Eng(self, "dve", nc.vector)
        self.pool = Eng(self, "pool", nc.gpsimd)
        self.sp = Eng(self, "sp", nc.sync)

    def new_sem(self, name):
        self.nsem += 1
        return self.stack.enter_context(self.nc.semaphore("s%d_%s" % (self.nsem, name)))

    def sbuf(self, name, shape, dtype):
        self.nbuf += 1
        t = self.stack.enter_context(self.nc.sbuf_tensor("%s_%d" % (name, self.nbuf), list(shape), dtype))
        return Buf(self, name, t)

    def psum(self, name, shape, dtype=F32):
        self.nbuf += 1
        t = self.stack.enter_context(self.nc.psum_tensor("%s_%d" % (name, self.nbuf), list(shape), dtype))
        return Buf(self, name, t)

    def dram(self, name, ap=None):
        return Buf(self, name, ap)

    def _collect(self, reads, writes):
        waits = {}

        def need(dep):
            if dep is None:
                return
            sem, val = dep
            cur = waits.get(id(sem))
            if cur is None or cur[1] < val:
                waits[id(sem)] = (sem, val)

        for b in reads:
            need(b.lw)
        for b in writes:
            need(b.lw)
            for dep in b.rd.values():
                need(dep)
        return waits

    def _emit_waits(self, E, waits, skip_own):
        for sem, val in waits.values():
            if skip_own and sem is E.sem:
                continue
            if E.seen.get(id(sem), 0) >= val:
                continue
            E.h.wait_ge(sem, val)
            E.seen[id(sem)] = val

    def op(self, E, fn, reads=(), writes=()):
        waits = self._collect(reads, writes)
        E.bump()
        self._emit_waits(E, waits, skip_own=(E.is_pe or not SAME_ENGINE_SYNC))
        ins = fn()
        E.cnt += 1
        ins.then_inc(E.sem, 1)
        tag = (E.sem, E.cnt)
        for b in reads:
            b.rd[id(E.sem)] = tag
        for b in writes:
            b.lw = tag
            b.rd = {}
        return ins

    def dma(self, Q, out_ap, in_ap, src, dst, sembuf):
        srcs = src if isinstance(src, (list, tuple)) else [src]
        dsts = dst if isinstance(dst, (list, tuple)) else [dst]
        waits = self._collect(srcs, dsts)
        dsem = sembuf.get_dsem()
        if sembuf.dcnt > 0:
            cur = waits.get(id(dsem))
            if cur is None or cur[1] < sembuf.dcnt:
                waits[id(dsem)] = (dsem, sembuf.dcnt)
        self._emit_waits(Q, waits, skip_own=False)
        ins = Q.h.dma_start(out=out_ap, in_=in_ap)
        sembuf.dcnt += 16
        ins.then_inc(dsem, 16)
        tag = (dsem, sembuf.dcnt)
        for b in srcs:
            b.rd[id(dsem)] = tag
        for b in dsts:
            b.lw = tag
            b.rd = {}
        return ins

    def barrier(self):
        engs = [self.pe, self.act, self.dve, self.pool, self.sp]
        deps = []
        for e in engs:
            if e.prev is not None:
                deps.append(e.prev)
            if e.cnt > 0:
                deps.append((e.sem, e.cnt))
        deps += [(b.dsem, b.dcnt) for b in self.dma_bufs if b.dcnt > 0]
        for E in engs:
            for sem, val in deps:
                if sem is E.sem:
                    continue
                if E.seen.get(id(sem), 0) >= val:
                    continue
                E.h.wait_ge(sem, val)
                E.seen[id(sem)] = val

    def finish(self, E, bufs):
        waits = self._collect(bufs, [])
        self._emit_waits(E, waits, skip_own=False)

    def close(self):
        self.stack.close()


def row_tiles(r0, r1):
    tiles = []
    r = r0
    span = r1 - r0
    lead = (span % 8) // 2
    nfront = 1 if lead >= 1 else 0
    nback = lead - nfront
    for _ in range(nfront):
        tiles.append((r, 2))
        r += 2
    while r + 8 <= r1 - 2 * nback:
        tiles.append((r, 8))
        r += 8
    for _ in range(nback):
        tiles.append((r, 2))
        r += 2
    assert r == r1, (r0, r1, tiles)
    return tiles


L0_Q_TILES = [(4, 2), (6, 8), (14, 8), (22, 8), (30, 8), (38, 8), (46, 2)]
L0_KV_TILES = [(0, 8), (8, 8), (16, 8), (24, 8), (32, 8), (40, 8), (48, 2)]
L1_TILES = [(4, 2), (6, 8), (14, 8), (22, 8), (30, 8), (38, 8)]
L2_Q_TILES = [(8, 2), (10, 8), (18, 8), (26, 8), (34, 8), (42, 2)]
L3_TILES = [(10, 8), (18, 8), (26, 8), (34, 8)]
POOL_IN = {1: (4, 47), 3: (8, 44)}


def key_pairs(R, nr):
    q0 = R // 2
    hi = q0 + (5 if nr == 8 else 2)
    return [k for k in range(q0 - 2, hi + 1) if 0 <= k <= FR // 2 - 1]


def build_nc(n_layers=DEPTH):
    nc = bass.Bass("TRN2", target_bir_lowering=False)
    fw = FW(nc)

    def din(name, shape, dt=F32):
        return nc.dram_tensor(name, list(shape), dt, kind="ExternalInput").ap()

    xT_in = din("xT", [D, T])
    cxT_in = din("cxT", [D, CTX])
    cs_in = din("cs", [128, KC, 2])
    ND, NNA, NPL = n_layers, (n_layers + 1) // 2, max(1, n_layers // 2)
    ada_w = din("ada_w", [ND, 24, 128, 8192])
    ada_b = din("ada_b", [ND, 128, 96])
    lnp = din("lnp", [ND, 128, 4, KC])
    psc = din("psc", [NPL, 128, KC])
    w_qkv = din("w_qkv", [NNA, 12, 128, 8192])
    w_o = din("w_o", [NNA, 4, 128, 8192])
    pool_w = din("pool_w", [NPL, 4, 128, 2048])
    w_in = din("w_in", [ND, 22, 128, 8192])
    w_out = din("w_out", [ND, 16, 128, 5632])
    toe_in = din("toe", [NNA, NH, 2, 128, NE * 128])
    rv_in = din("rv", [2, 7 * 8, 128, 512], BF16)
    pm_in = din("pm", [128, 5, T])
    pmc_in = din("pmc", [128, 5, CTX])
    ident_in = din("ident", [128, 128], BF16)
    outT = nc.dram_tensor("outT", [D, 32 * GW], F32, kind="ExternalOutput").ap()

    qkvT = nc.dram_tensor("qkvT", [48, 128, T], BF16).ap()
    qkvcT = nc.dram_tensor("qkvcT", [48, 128, CTX], BF16).ap()
    oT = nc.dram_tensor("oT", [KC, 128, T], BF16).ap()
    ocT = nc.dram_tensor("ocT", [KC, 128, CTX], BF16).ap()
    xT = nc.dram_tensor("xTs", [D, T], F32).ap()
    cxT = nc.dram_tensor("cxTs", [D, CTX], F32).ap()

    d_const = fw.dram("const")
    d_x = fw.dram("xT")
    d_cx = fw.dram("cxT")
    d_qkv = fw.dram("qkvT")
    d_qkvc = fw.dram("qkvcT")
    d_o = fw.dram("oT")
    d_oc = fw.dram("ocT")
    d_out = fw.dram("outT")

    xs = fw.sbuf("xs", [128, KC, 512], F32)
    hs = fw.sbuf("hs", [128, KC, 512], BF16)
    hs2 = hs
    ub = fw.sbuf("ub", [128, FC, 512], BF16)
    wbufs = [fw.sbuf("wb%d" % i, [128, 8192], BF16) for i in range(2)]
    stage = [fw.sbuf("stg%d" % i, [128, 512], BF16) for i in range(2)]
    sgb = [fw.sbuf("sg%d" % i, [128, 512], F32) for i in range(2)]
    zb = [fw.sbuf("zb%d" % i, [128, 512], BF16) for i in range(3)]
    zq = [fw.sbuf("zq%d" % i, [128, 512], BF16) for i in range(3)]
    mean_t = fw.sbuf("mean", [128, 512], F32)
    rstd_t = fw.sbuf("rstd", [128, 512], F32)
    ones_bf = fw.sbuf("ones", [128, 128], BF16)
    ident = fw.sbuf("ident", [128, 128], BF16)
    eps_t = fw.sbuf("eps", [128, 1], F32)
    cs_t = fw.sbuf("cs", [128, KC, 2], F32)
    csb = fw.sbuf("csb", [128, KC, 2], BF16)
    modvs = [fw.sbuf("modv%d" % i, [128, 96, 2], F32) for i in range(2)]
    adabs = [fw.sbuf("adab%d" % i, [128, 96], F32) for i in range(2)]
    LP = []
    for i in range(n_layers):
        LP.append({"modv": modvs[i % 2], "adab": adabs[i % 2], "lnp": fw.sbuf("lnp%d" % i, [128, 4, KC], F32),
                   "psc": fw.sbuf("psc%d" % i, [128, KC], F32),
                   "prm": [fw.sbuf("prm%d_%d" % (i, j), [128, 6, KC], F32) for j in range(2)]})
    lnp_t = LP[0]["lnp"]
    prm = LP[0]["prm"]
    NA_EL = 4 * T + 4 * CTX + 2 * NE * 128
    attA = fw.sbuf("attA", [128, NA_EL], BF16)
    pb = [fw.sbuf("pb%d" % i, [128, 512], BF16) for i in range(4)]
    rvb = [fw.sbuf("rvb%d" % i, [128, 512], BF16) for i in range(4)]
    rDs = [mean_t, rstd_t]
    pmk = fw.sbuf("pmk", [128, 5, 512], F32)

    banks = [fw.psum("bk%d" % i, [128, 512], F32) for i in range(7)]
    pst = fw.psum("pst", [128, 1024], BF16)
    mm = banks[0:3]
    st_sum, st_sq = banks[3], banks[4]
    psOs = [banks[5], banks[3]]
    psDs = [banks[6], banks[4]]

    ubf = ub.ap[:].rearrange("p a b -> p (a b)")
    xsf = xs.ap[:].rearrange("p a b -> p (a b)")
    stage4 = [Buf(fw, "st4_%d" % i, ubf[:, i * 2048:(i + 1) * 2048].rearrange("p (a b) -> p a b", a=4)) for i in range(4)]

    def mkset(name, v, toefs):
        o = [0]

        def take(n):
            r = v[:, o[0]:o[0] + n]
            o[0] += n
            return r
        S = {}
        S["qt"] = Buf(fw, name + "qt", take(T))
        S["kt"] = Buf(fw, name + "kt", take(T))
        S["vT"] = Buf(fw, name + "vT", take(T))
        S["vv"] = Buf(fw, name + "vv", take(T).rearrange("p (a b) -> p a b", b=128))
        S["kct"] = Buf(fw, name + "kct", take(CTX))
        S["vcT"] = Buf(fw, name + "vcT", take(CTX))
        S["vc"] = Buf(fw, name + "vc", take(CTX).rearrange("p (a b) -> p a b", b=128))
        S["qct"] = Buf(fw, name + "qct", take(CTX))
        S["toeb"] = Buf(fw, name + "toeb", take(2 * NE * 128).rearrange("p (a b) -> p a b", a=2))
        S["toef"] = [Buf(fw, name + "toef%d" % i, t) for i, t in enumerate(toefs)]
        return S
    wbufs.append(Buf(fw, "wb2", attA.ap[:, 0:8192]))
    nwb = [3]
    n1 = NE * 128
    setA = mkset("A", attA.ap, [xsf[:, 0:n1], xsf[:, n1:2 * n1]])
    setB = mkset("B", ubf, [xsf[:, 2 * n1:3 * n1], xsf[:, 3 * n1:4 * n1]])
    xs_c = [Buf(fw, "xs%d" % m) for m in range(KC)]
    hs_c = [Buf(fw, "hs%d" % m) for m in range(KC)]
    ub_c = [Buf(fw, "ub%d" % j) for j in range(FC)]
    pe, act, dve, pool, sp = fw.pe, fw.act, fw.dve, fw.pool, fw.sp
    cnt = {"mm": 0, "w": 0, "stg": 0, "ev": 0, "pb": 0, "z": 0, "sg": 0, "rv": 0, "st4": 0, "rd": 0}

    def nxt(key, n):
        v = cnt[key]
        cnt[key] = (v + 1) % n
        return v

    fw.op(dve, lambda: nc.vector.memset(ones_bf.ap[:], 1.0), writes=[ones_bf])
    fw.op(dve, lambda: nc.vector.memset(eps_t.ap[:], EPS2), writes=[eps_t])
    fw.dma(sp, ident.ap[:], ident_in, d_const, ident, ident)
    fw.dma(sp, cs_t.ap[:], cs_in, d_const, cs_t, cs_t)
    fw.op(act, lambda: nc.scalar.activation(out=csb.ap[:], in_=cs_t.ap[:], func=AF.Silu), reads=[cs_t], writes=[csb])

    def load_w(Wt_g, kc, tot):
        wb = wbufs[nxt("w", nwb[0])]
        fw.dma(pool, wb.ap[:, 0:kc * tot], Wt_g, d_const, wb, wb)
        view = wb.ap[:, 0:kc * tot].rearrange("p (c m) -> p c m", c=kc)
        return wb, view

    def linear_gen(h_buf, h_view, kc, N, Wt, g_list, gsz, evac):
        for gpos, g in enumerate(g_list):
            wb, wv = load_w(Wt[g], kc, gsz * 128)
            for gi in range(gsz):
                bk = mm[nxt("mm", 3)]
                for k in range(kc):
                    fw.op(pe, lambda k=k, gi=gi, bk=bk: nc.tensor.matmul(
                        bk.ap[:, 0:N], lhsT=wv[:, k, gi * 128:(gi + 1) * 128], rhs=h_view(k),
                        start=(k == 0), stop=(k == kc - 1)), reads=[wb, h_buf(k)], writes=[bk])
                evac(gpos * gsz + gi, bk, bk.ap[:, 0:N])
            yield

    def linear(*a):
        for _ in linear_gen(*a):
            pass

    def evac_copy(dst_view, src_buf, src_view, dst_buf):
        if nxt("ev", 2) == 0:
            fw.op(act, lambda: nc.scalar.copy(out=dst_view, in_=src_view), reads=[src_buf], writes=[dst_buf])
        else:
            fw.op(dve, lambda: nc.vector.tensor_copy(out=dst_view, in_=src_view), reads=[src_buf], writes=[dst_buf])

    def load_x(xT_ap, dx, t0, N):
        for c in range(KC):
            fw.dma(sp, xs.ap[:, c, 0:N], xT_ap[c * 128:(c + 1) * 128, t0:t0 + N], dx, xs_c[c], xs_c[c])

    def store_x(xT_ap, dx, t0, N):
        for c in range(KC):
            fw.dma(sp, xT_ap[c * 128:(c + 1) * 128, t0:t0 + N], xs.ap[:, c, 0:N], xs_c[c], dx, xs_c[c])

    def modulate(P, ia, ib, N, out_buf):
        for c in range(KC):
            fw.op(act, lambda c=c: nc.scalar.activation(out=hs.ap[:, c, 0:N], in_=xs.ap[:, c, 0:N], func=AF.Identity,
                                                        scale=P.ap[:, ia, c:c + 1], bias=P.ap[:, ib, c:c + 1]),
                  reads=[xs_c[c], P], writes=[hs_c[c]])

    def resid_evac(segs, iga, N):
        pend = []

        def stats(m, i):
            fw.op(pe, lambda: nc.tensor.matmul(st_sum.ap[:, 0:N], lhsT=ones_bf.ap[:], rhs=zb[i].ap[:, 0:N],
                                               start=(m == 0), stop=(m == KC - 1)), reads=[ones_bf, zb[i]], writes=[st_sum])
            fw.op(pe, lambda: nc.tensor.matmul(st_sq.ap[:, 0:N], lhsT=ones_bf.ap[:], rhs=zq[i].ap[:, 0:N],
                                               start=(m == 0), stop=(m == KC - 1)), reads=[ones_bf, zq[i]], writes=[st_sq])

        def ev(m, bk, bv):
            if pend:
                stats(*pend.pop())
            if m is None:
                return
            for (o_, n_, P) in segs:
                fw.op(dve, lambda: nc.vector.scalar_tensor_tensor(out=xs.ap[:, m, o_:o_ + n_], in0=bv[:, o_:o_ + n_],
                                                                  scalar=P.ap[:, iga, m:m + 1], in1=xs.ap[:, m, o_:o_ + n_],
                                                                  op0=ALU.mult, op1=ALU.add),
                      reads=[bk, P, xs_c[m]], writes=[xs_c[m]])
            i = nxt("z", 3)
            fw.op(act, lambda: nc.scalar.copy(out=zb[i].ap[:, 0:N], in_=xs.ap[:, m, 0:N]), reads=[xs_c[m]], writes=[zb[i]])
            fw.op(act, lambda: nc.scalar.activation(out=zq[i].ap[:, 0:N], in_=xs.ap[:, m, 0:N], func=AF.Square),
                  reads=[xs_c[m]], writes=[zq[i]])
            pend.append((m, i))
        return ev

    def ln_finish(segs, N, ig, ib, ia2=None, ib2=None):
        fw.op(act, lambda: nc.scalar.mul(out=mean_t.ap[:, 0:N], in_=st_sum.ap[:, 0:N], mul=1.0 / D),
              reads=[st_sum], writes=[mean_t])
        fw.op(dve, lambda: nc.vector.tensor_tensor(out=rstd_t.ap[:, 0:N], in0=mean_t.ap[:, 0:N], in1=mean_t.ap[:, 0:N], op=ALU.mult),
              reads=[mean_t], writes=[rstd_t])
        fw.op(dve, lambda: nc.vector.scalar_tensor_tensor(out=rstd_t.ap[:, 0:N], in0=st_sq.ap[:, 0:N], scalar=1.0 / D,
                                                          in1=rstd_t.ap[:, 0:N], op0=ALU.mult, op1=ALU.subtract),
              reads=[st_sq, rstd_t], writes=[rstd_t])
        fw.op(act, lambda: nc.scalar.activation(out=rstd_t.ap[:, 0:N], in_=rstd_t.ap[:, 0:N], func=AF.Ln, bias=eps_t.ap[:, 0:1]),
              reads=[rstd_t, eps_t], writes=[rstd_t])
        fw.op(act, lambda: nc.scalar.activation(out=rstd_t.ap[:, 0:N], in_=rstd_t.ap[:, 0:N], func=AF.Exp, scale=-0.5),
              reads=[rstd_t], writes=[rstd_t])
        def xnew(m):
            xv = xs.ap[:, m, 0:N]
            fw.op(act, lambda: nc.scalar.activation(out=xv, in_=xv, func=AF.Identity, bias=lnp_t.ap[:, ib, m:m + 1]),
                  reads=[xs_c[m], lnp_t], writes=[xs_c[m]])
        for m in range(KC):
            xv = xs.ap[:, m, 0:N]
            fw.op(dve, lambda xv=xv: nc.vector.tensor_tensor(out=xv, in0=xv, in1=mean_t.ap[:, 0:N], op=ALU.subtract),
                  reads=[xs_c[m], mean_t], writes=[xs_c[m]])
            fw.op(dve, lambda xv=xv, m=m: nc.vector.scalar_tensor_tensor(out=xv, in0=xv, scalar=lnp_t.ap[:, ig, m:m + 1],
                                                                         in1=rstd_t.ap[:, 0:N], op0=ALU.mult, op1=ALU.mult),
                  reads=[xs_c[m], lnp_t, rstd_t], writes=[xs_c[m]])
            if ia2 is not None:
                for (o_, n_, P) in segs:
                    fw.op(act, lambda: nc.scalar.activation(out=hs2.ap[:, m, o_:o_ + n_], in_=xs.ap[:, m, o_:o_ + n_],
                                                            func=AF.Identity, scale=P.ap[:, ia2, m:m + 1], bias=P.ap[:, ib2, m:m + 1]),
                          reads=[xs_c[m], P], writes=[hs_c[m]])
                if m >= 1:
                    xnew(m - 1)
            else:
                xnew(m)
        if ia2 is not None:
            xnew(KC - 1)

    def ffn(li, segs, N, mid_hook=None):
        W1 = w_in[li]
        W2 = w_out[li]
        state = {}

        def ev(i, bk, bv):
            q, r = divmod(i, 4)
            j = 2 * q + (r % 2)
            if r < 2:
                s = sgb[nxt("sg", 2)]
                fw.op(act, lambda: nc.scalar.activation(out=s.ap[:, 0:N], in_=bv, func=AF.Silu), reads=[bk], writes=[s])
                state[j] = s
            else:
                s = state.pop(j)
                fw.op(dve, lambda: nc.vector.tensor_tensor(out=ub.ap[:, j, 0:N], in0=bv, in1=s.ap[:, 0:N], op=ALU.mult),
                      reads=[bk, s], writes=[ub_c[j]])
        linear(lambda k: hs_c[k], lambda k: hs2.ap[:, k, 0:N], KC, N, W1, list(range(22)), 4, ev)
        if mid_hook is not None:
            mid_hook()
        rev = resid_evac(segs, 5, N)
        linear(lambda k: ub_c[k], lambda k: ub.ap[:, k, 0:N], FC, N, W2, list(range(KC)), 1, rev)
        rev(None, None, None)
        ln_finish(segs, N, 2, 3)

    def ada_gen(li):
        use_na_ = li % 2 == 0
        use_ctx_ = (li < 2) or use_na_
        pool_layer = not use_na_
        pj = li // 2
        lp = LP[li]
        modv, adab, lnp_l, psc_l = lp["modv"], lp["adab"], lp["lnp"], lp["psc"]
        fw.dma(sp, adab.ap[:], ada_b[li], d_const, adab, adab)
        fw.dma(sp, lnp_l.ap[:], lnp[li], d_const, lnp_l, lnp_l)
        if pool_layer:
            fw.dma(sp, psc_l.ap[:], psc[pj], d_const, psc_l, psc_l)

        def ev(i, bk, bv):
            fw.op(dve, lambda: nc.vector.tensor_scalar_add(out=modv.ap[:, i, :], in0=bk.ap[:, 0:2], scalar1=adab.ap[:, i:i + 1]),
                  reads=[bk, adab], writes=[modv])
        yield from linear_gen(lambda k: csb, lambda k: csb.ap[:, k, :], KC, 2, ada_w[li], list(range(24)), 4, ev)
        for j in range(2 if use_ctx_ else 1):
            P = lp["prm"][j]
            sh1, sc1, g1 = modv.ap[:, 0:16, j], modv.ap[:, 16:32, j], modv.ap[:, 32:48, j]
            sh2, sc2, g2 = modv.ap[:, 48:64, j], modv.ap[:, 64:80, j], modv.ap[:, 80:96, j]
            o = lambda fn: fw.op(dve, fn, reads=[modv, lnp_l, psc_l, P], writes=[P])
            o(lambda: nc.vector.tensor_scalar_add(out=P.ap[:, 0, :], in0=sc1, scalar1=1.0))
            o(lambda: nc.vector.tensor_copy(out=P.ap[:, 1, :], in_=sh1))
            o(lambda: nc.vector.tensor_scalar_mul(out=P.ap[:, 2, :], in0=g1, scalar1=1.0 / ALPHA))
            if pool_layer:
                o(lambda: nc.vector.tensor_tensor(out=P.ap[:, 2, :], in0=P.ap[:, 2, :], in1=psc_l.ap[:], op=ALU.mult))
            o(lambda: nc.vector.tensor_scalar_add(out=P.ap[:, 3, :], in0=sc2, scalar1=1.0))
            o(lambda: nc.vector.tensor_tensor(out=P.ap[:, 4, :], in0=P.ap[:, 3, :], in1=lnp_l.ap[:, 1, :], op=ALU.mult))
            o(lambda: nc.vector.tensor_tensor(out=P.ap[:, 4, :], in0=P.ap[:, 4, :], in1=sh2, op=ALU.add))
            o(lambda: nc.vector.tensor_scalar_mul(out=P.ap[:, 5, :], in0=g2, scalar1=1.0 / ALPHA))
        yield

    def chain(gens):
        for g in gens:
            yield from g

    def pull(bg, n):
        if bg is None:
            return
        for _ in range(n):
            try:
                next(bg)
            except StopIteration:
                return

    def qkv_phase(ni, xT_ap, dx, tiles_tok, P, dst_ap, ddst, chunk_list):
        Wq = w_qkv[ni]
        for (t0, N) in tiles_tok:
            load_x(xT_ap, dx, t0, N)
            modulate(P, 0, 1, N, hs)
            cur = {}

            def ev(i, bk, bv, t0=t0, N=N):
                gi = i % 4
                if gi == 0:
                    cur["b"] = stage4[nxt("st4", 4)]
                sb_ = cur["b"]
                evac_copy(sb_.ap[:, gi, 0:N], bk, bv, sb_)
                if gi == 3:
                    c0 = chunk_list[i - 3]
                    fw.dma(sp, dst_ap[c0:c0 + 4, :, t0:t0 + N].rearrange("c p t -> p c t"), sb_.ap[:, :, 0:N], sb_, ddst, sb_)
            linear(lambda k: hs_c[k], lambda k, N=N: hs.ap[:, k, 0:N], KC, N, Wq, [c // 4 for c in chunk_list[::4]], 4, ev)

    def transpose_v(src, dst, npairs):
        for p0 in range(0, npairs, 8):
            n = min(8, npairs - p0)
            for i in range(n):
                fw.op(pe, lambda i=i, p0=p0: nc.tensor.transpose(out=pst.ap[:, i * 128:(i + 1) * 128],
                                                                 in_=src.ap[:, (p0 + i) * 128:(p0 + i + 1) * 128],
                                                                 identity=ident.ap[:]), reads=[src, ident], writes=[pst])
            fw.op(dve, lambda p0=p0, n=n: nc.vector.tensor_copy(
                out=dst.ap[:, p0:p0 + n, :], in_=pst.ap[:, 0:n * 128].rearrange("p (a b) -> p a b", b=128)),
                reads=[pst], writes=[dst])

    LA = 2
    att = {"o": 0}

    def attend_tiles(tiles, tile_hook=None):
        steps = []
        for t in tiles:
            oi = att["o"]
            att["o"] = 1 - oi
            t["pO"], t["pD"] = psOs[oi], psDs[oi]
            nk = len(t["keys"])
            for ki in range(nk):
                steps.append((t, ki, nk))
        pbufs = {}
        for i in range(len(steps) + LA):
            if i < len(steps):
                t, ki, nk = steps[i]
                Nq, qbuf, qview = t["Nq"], t["qbuf"], t["qview"]
                kv_, kb_, vv_, vb_, toe_v, toe_b, rv_v, rv_b = t["keys"][ki]
                bk = mm[nxt("mm", 3)]
                fw.op(pe, lambda: nc.tensor.matmul(bk.ap[:, 0:Nq], lhsT=kv_, rhs=qview, start=True, stop=True),
                      reads=[kb_, qbuf], writes=[bk])
                p_ = pb[nxt("pb", 4)]
                fw.op(act, lambda: nc.scalar.activation(out=p_.ap[:, 0:Nq], in_=bk.ap[:, 0:Nq], func=AF.Exp, scale=SCALE),
                      reads=[bk], writes=[p_])
                if toe_v is not None:
                    fw.op(dve, lambda: nc.vector.tensor_tensor(out=p_.ap[:, 0:Nq], in0=p_.ap[:, 0:Nq], in1=toe_v, op=ALU.mult),
                          reads=[p_, toe_b], writes=[p_])
                if rv_v is not None:
                    fw.op(pool, lambda: nc.gpsimd.tensor_tensor(out=p_.ap[:, 0:Nq], in0=p_.ap[:, 0:Nq], in1=rv_v, op=ALU.mult),
                          reads=[p_, rv_b], writes=[p_])
                pbufs[i] = p_
            j = i - LA
            if j >= 0:
                t, ki, nk = steps[j]
                Nq, pO, pD = t["Nq"], t["pO"], t["pD"]
                vv_, vb_ = t["keys"][ki][2], t["keys"][ki][3]
                p_ = pbufs.pop(j)
                fw.op(pe, lambda: nc.tensor.matmul(pO.ap[:, 0:Nq], lhsT=vv_, rhs=p_.ap[:, 0:Nq], start=(ki == 0), stop=(ki == nk - 1)),
                      reads=[vb_, p_], writes=[pO])
                fw.op(pe, lambda: nc.tensor.matmul(pD.ap[:, 0:Nq], lhsT=ones_bf.ap[:], rhs=p_.ap[:, 0:Nq], start=(ki == 0),
                                                   stop=(ki == nk - 1)), reads=[ones_bf, p_], writes=[pD])
                if ki == nk - 1:
                    rD_ = rDs[nxt("rd", 2)]
                    fw.op(act, lambda: nc.scalar.activation(out=rD_.ap[:, 0:Nq], in_=pD.ap[:, 0:Nq], func=AF.Ln), reads=[pD], writes=[rD_])
                    fw.op(act, lambda: nc.scalar.activation(out=rD_.ap[:, 0:Nq], in_=rD_.ap[:, 0:Nq], func=AF.Exp, scale=-1.0),
                          reads=[rD_], writes=[rD_])
                    sg_ = stage[nxt("stg", 2)]
                    fw.op(dve, lambda: nc.vector.tensor_tensor(out=sg_.ap[:, 0:Nq], in0=pO.ap[:, 0:Nq], in1=rD_.ap[:, 0:Nq], op=ALU.mult),
                          reads=[pO, rD_], writes=[sg_])
                    fw.dma(sp, t["out_ap"], sg_.ap[:, 0:Nq], sg_, t["dout"], sg_)
                    if tile_hook is not None:
                        tile_hook()

    def attention_phase(ni, q_tiles, with_ctx_q, edge_tiles, bg, n_bg):
        sets = [setA, setB]

        def load_dmas(h, S):
            fw.dma(sp, S["qt"].ap, qkvT[h], d_qkv, S["qt"], S["qt"])
            fw.dma(sp, S["kt"].ap, qkvT[16 + h], d_qkv, S["kt"], S["kt"])
            fw.dma(sp, S["vT"].ap, qkvT[32 + h], d_qkv, S["vT"], S["vT"])
            fw.dma(sp, S["kct"].ap, qkvcT[16 + h], d_qkvc, S["kct"], S["kct"])
            fw.dma(sp, S["vcT"].ap, qkvcT[32 + h], d_qkvc, S["vcT"], S["vcT"])
            if with_ctx_q:
                fw.dma(sp, S["qct"].ap, qkvcT[h], d_qkvc, S["qct"], S["qct"])
            for w_ in range(2):
                fw.dma(sp, S["toef"][w_].ap, toe_in[ni, h, w_], d_const, S["toef"][w_], S["toef"][w_])

        def late(S):
            for w_ in range(2):
                fw.op(act, lambda w_=w_: nc.scalar.activation(out=S["toeb"].ap[:, w_, :], in_=S["toef"][w_].ap, func=AF.Exp),
                      reads=[S["toef"][w_]], writes=[S["toeb"]])
            transpose_v(S["vT"], S["vv"], FR // 2)
            transpose_v(S["vcT"], S["vc"], 2)

        hsf = hs.ap[:].rearrange("p a b -> p (a b)")
        rvres = Buf(fw, "rvres", hsf.rearrange("p (k t) -> p k t", t=512))
        e_idx = {}
        for ei, et in enumerate(edge_tiles):
            ti = q_tiles.index(et)
            e_idx[et] = ei
            fw.dma(sp, rvres.ap[:, ei * 8:(ei + 1) * 8, :], rv_in[ni, ti * 8:(ti + 1) * 8].rearrange("k p t -> p k t"),
                   d_const, rvres, rvres)
        tcount = [0]

        def tile_hook():
            tcount[0] += 1
            if tcount[0] % n_bg == 0:
                pull(bg, 1)
        load_dmas(0, sets[0])
        for h in range(NH):
            S = sets[h % 2]
            if h + 1 < NH:
                load_dmas(h + 1, sets[(h + 1) % 2])
            late(S)
            qt_, kt_, vv_, kct_, vc_, toeb_ = S["qt"], S["kt"], S["vv"], S["kct"], S["vc"], S["toeb"]
            ckeys = [(kct_.ap[:, i * 128:(i + 1) * 128], kct_, vc_.ap[:, i, :], vc_, None, None, None, None) for i in range(2)]
            tiles = []
            for ti, (R, nr) in enumerate(q_tiles):
                Nq = nr * GW
                q0 = R // 2
                edge = (R, nr) in edge_tiles
                keys = []
                for kk, kp in enumerate(key_pairs(R, nr)):
                    e0 = 5 - (kp - q0)
                    keys.append((kt_.ap[:, kp * 128:(kp + 1) * 128], kt_, vv_.ap[:, kp, :], vv_,
                                 toeb_.ap[:, 1 if edge else 0, e0 * 128:e0 * 128 + Nq], toeb_,
                                 rvres.ap[:, e_idx[(R, nr)] * 8 + kk, 0:Nq] if edge else None, rvres))
                tiles.append(dict(qbuf=qt_, qview=qt_.ap[:, R * GW:R * GW + Nq], Nq=Nq, keys=keys + ckeys,
                                  out_ap=oT[h, :, R * GW:R * GW + Nq], dout=d_o))
            if with_ctx_q:
                tiles.append(dict(qbuf=S["qct"], qview=S["qct"].ap, Nq=CTX, keys=ckeys, out_ap=ocT[h], dout=d_oc))
            attend_tiles(tiles, tile_hook)

    def mixer_out_phase(li, tiles, wfn):
        def offs(tile):
            o, r = 0, []
            for sg in tile:
                r.append((o, sg))
                o += sg["n"]
            return r, o

        def load_mix(tile, Q):
            for o, sg in offs(tile)[0]:
                for c in range(KC):
                    fw.dma(Q, hs.ap[:, c, o:o + sg["n"]], sg["src"][c, :, sg["t0"]:sg["t0"] + sg["n"]], sg["dsrc"], hs_c[c], hs_c[c])
        for i, tile in enumerate(tiles):
            so, N = offs(tile)
            segs = [(o, sg["n"], sg["P"]) for o, sg in so]
            if i == 0:
                load_mix(tile, sp)
            for o, sg in so:
                for c in range(KC):
                    fw.dma(sp, xs.ap[:, c, o:o + sg["n"]], sg["xl"][c * 128:(c + 1) * 128, sg["t0"]:sg["t0"] + sg["n"]],
                           sg["dxl"], xs_c[c], xs_c[c])
            wfn(N, resid_evac(segs, 2, N))
            ln_finish(segs, N, 0, 1, 3, 4)
            hook = None
            if i + 1 < len(tiles):
                hook = lambda nt=tiles[i + 1]: load_mix(nt, act)
            ffn(li, segs, N, hook)
            for o, sg in so:
                for c in range(KC):
                    fw.dma(sp, sg["xst"][c * 128:(c + 1) * 128, sg["t0"]:sg["t0"] + sg["n"]], xs.ap[:, c, o:o + sg["n"]],
                           xs_c[c], sg["dxst"], xs_c[c])
                if sg["fo"] is not None:
                    sg["fo"](sg["t0"], sg["n"], o)

    def mk_tiles(lat_tiles, lat, ctx):
        full = [[dict(lat, t0=t0, n=n)] for (t0, n) in lat_tiles if n == 512]
        merged = [dict(lat, t0=t0, n=n) for (t0, n) in lat_tiles if n != 512]
        if ctx is not None:
            merged.append(dict(ctx, t0=0, n=CTX))
        assert sum(sg["n"] for sg in merged) <= 512
        return ([merged] if merged else []) + full

    def tok(tiles):
        return [(R * GW, nr * GW) for (R, nr) in tiles]

    def na_layer(li, ni, kv_tiles, q_tiles, update_ctx, xsrc, cxsrc, dxl, dcl, edge_tiles, bg, n_bg):
        Pl, Pc = prm[0], prm[1]
        allc = list(range(48))
        fw.barrier()
        qkv_phase(ni, xsrc, dxl, tok(kv_tiles), Pl, qkvT, d_qkv, allc)
        qkv_phase(ni, cxsrc, dcl, [(0, CTX)], Pc, qkvcT, d_qkvc, allc if update_ctx else list(range(16, 48)))
        fw.barrier()
        nwb[0] = 2
        cnt["w"] = 0
        attention_phase(ni, q_tiles, update_ctx, edge_tiles, bg, n_bg)
        pull(bg, 10 ** 6)
        fw.barrier()
        nwb[0] = 3
        Wo = w_o[ni]

        def wfn(N, ev):
            linear(lambda k: hs_c[k], lambda k: hs.ap[:, k, 0:N], KC, N, Wo, list(range(4)), 4, ev)
            ev(None, None, None)
        return wfn

    def pool_phase(xT_ap, dx, pm_ap, t_lo, t_hi, o_lo, o_hi, P, dst_ap, ddst, bg=None):
        for s0 in range(o_lo, o_hi, 496):
            N = min(496, o_hi - s0)
            a = max(t_lo, s0 - 8)
            b = min(t_hi, s0 + N + 8)
            off = a - (s0 - 8)
            W_ = N + 16
            n_in = b - a
            fw.dma(sp, pmk.ap[:, :, off:off + n_in], pm_ap[:, :, a:b], d_const, pmk, pmk)
            def prefetch(c):
                b0 = (c % 2) * 8
                hm = xs.ap[:, b0 + 1, :]
                xin = xs.ap[:, b0 + 4, :]
                eh = nc.vector if c % 2 == 0 else nc.gpsimd
                fw.op(dve if c % 2 == 0 else pool, lambda: eh.memset(hm, 0.0), writes=[xs_c[b0 + 1]])
                fw.dma(sp, xin[:, off:off + n_in], xT_ap[c * 128:(c + 1) * 128, a:b], dx, xs_c[b0 + 4], xs_c[b0 + 4])
            prefetch(0)
            for c in range(KC):
                g = c // 4
                b0 = (c % 2) * 8
                hA, hm, s1, s2, xin, hAw = [xs.ap[:, b0 + i, :] for i in range(6)]
                tA, tm, t1, t2, tin, tw = [xs_c[b0 + i] for i in range(6)]
                eng = dve if c % 2 == 0 else pool
                eh = nc.vector if c % 2 == 0 else nc.gpsimd
                if c + 1 < KC:
                    prefetch(c + 1)
                fw.op(act, lambda: nc.scalar.activation(out=hA[:, off:off + n_in], in_=xin[:, off:off + n_in], func=AF.Identity,
                                                        scale=P.ap[:, 0, c:c + 1], bias=P.ap[:, 1, c:c + 1]),
                      reads=[tin, P], writes=[tA])
                fw.op(eng, lambda: eh.tensor_tensor(out=hm[:, off:off + n_in], in0=hA[:, off:off + n_in],
                                                    in1=pmk.ap[:, 0, off:off + n_in], op=ALU.mult), reads=[tA, pmk], writes=[tm])
                fw.op(eng, lambda: eh.tensor_tensor(out=s1[:, 1:W_], in0=hm[:, 0:W_ - 1], in1=hm[:, 1:W_], op=ALU.add),
                      reads=[tm], writes=[t1])
                cur, other, tcur, tother = s1, s2, t1, t2
                sh = 1
                for _ in range(g):
                    fw.op(eng, lambda cur=cur, other=other, sh=sh: eh.tensor_tensor(
                        out=other[:, sh:W_ - sh], in0=cur[:, 0:W_ - 2 * sh], in1=cur[:, 2 * sh:W_], op=ALU.add),
                        reads=[tcur], writes=[tother])
                    cur, other, tcur, tother = other, cur, tother, tcur
                    sh *= 2
                fw.op(eng, lambda cur=cur: eh.tensor_tensor(out=hAw[:, 8:8 + N], in0=cur[:, 8:8 + N],
                                                            in1=pmk.ap[:, 1 + g, 8:8 + N], op=ALU.mult), reads=[tcur, pmk], writes=[tw])
                sg_ = stage[nxt("stg", 2)]
                fw.op(eng, lambda: eh.tensor_tensor(out=sg_.ap[:, 0:N], in0=hAw[:, 8:8 + N], in1=hA[:, 8:8 + N], op=ALU.subtract),
                      reads=[tw, tA], writes=[sg_])
                fw.dma(sp, dst_ap[c, :, s0:s0 + N], sg_.ap[:, 0:N], sg_, ddst, sg_)
                if c % 3 == 2:
                    pull(bg, 1)
        pull(bg, 10 ** 6)

    def pool_wfn(pj):
        def wfn(N, ev):
            for g in range(4):
                linear(lambda k, g=g: hs_c[4 * g + k], lambda k, g=g: hs.ap[:, 4 * g + k, 0:N], 4, N, pool_w[pj], [g], 4,
                       lambda i, bk, bv, g=g: ev(4 * g + i, bk, bv))
            ev(None, None, None)
        return wfn

    last_na = 2
    final_bufs = [d_out]
    for li in range(n_layers):
        use_na = li % 2 == 0
        jj = li // 2
        update_ctx = li < last_na
        use_ctx = update_ctx or use_na
        xsrc = xT_in if li == 0 else xT
        cxsrc = cxT_in if li == 0 else cxT
        if li == 0:
            pull(ada_gen(0), 10 ** 6)
        lnp_t = LP[li]["lnp"]
        prm = LP[li]["prm"]
        is_last = li == n_layers - 1

        def final_out(t0, N, o):
            lo = max(t0, OWN0 * GW)
            hi = min(t0 + N, (OWN0 + 32) * GW)
            if hi > lo:
                for c in range(KC):
                    fw.dma(sp, outT[c * 128:(c + 1) * 128, lo - OWN0 * GW:hi - OWN0 * GW],
                           xs.ap[:, c, o + lo - t0:o + hi - t0], xs_c[c], d_out, xs_c[c])

        fo = final_out if is_last else None
        dxl = d_const if li == 0 else d_x
        dcl = d_const if li == 0 else d_cx
        if use_na:
            kv_tiles, q_tiles = (L0_KV_TILES, L0_Q_TILES) if li == 0 else (L1_TILES, L2_Q_TILES)
            edge_tiles = [(6, 8), (38, 8)] if li == 0 else [(10, 8), (34, 8)]
            nxt_layers = [l for l in ((1,) if li == 0 else (3,)) if l < n_layers]
            bg = chain([ada_gen(l) for l in nxt_layers])
            wfn = na_layer(li, jj, kv_tiles, q_tiles, update_ctx, xsrc, cxsrc, dxl, dcl, edge_tiles, bg, 4 if li == 0 else 3)
            lat = dict(src=oT, dsrc=d_o, xl=xsrc, dxl=dxl, xst=xT, dxst=d_x, P=prm[0], fo=fo)
            ctxs = dict(src=ocT, dsrc=d_oc, xl=cxsrc, dxl=dcl, xst=cxT, dxst=d_cx, P=prm[1], fo=None) if update_ctx else None
            mixer_out_phase(li, mk_tiles(tok(q_tiles), lat, ctxs), wfn)
        else:
            tiles = L1_TILES if li == 1 else L3_TILES
            a, b = POOL_IN[li]
            o_lo, o_hi = tiles[0][0] * GW, (tiles[-1][0] + tiles[-1][1]) * GW
            bgp = ada_gen(2) if (li == 1 and n_layers > 2) else None
            pool_phase(xT, d_x, pm_in, a * GW, b * GW, o_lo, o_hi, prm[0], oT, d_o, bgp)
            if update_ctx:
                pool_phase(cxT, d_cx, pmc_in, 0, CTX, 0, CTX, prm[1], ocT, d_oc)
            lat = dict(src=oT, dsrc=d_o, xl=xT, dxl=d_x, xst=xT, dxst=d_x, P=prm[0], fo=fo)
            ctxs = dict(src=ocT, dsrc=d_oc, xl=cxT, dxl=d_cx, xst=cxT, dxst=d_cx, P=prm[1], fo=None) if update_ctx else None
            mixer_out_phase(li, mk_tiles(tok(tiles), lat, ctxs), pool_wfn(jj))
    fw.finish(sp, final_bufs)
    fw.close()
    return nc


def _fm(v):
    return np.ascontiguousarray(np.swapaxes(v.reshape(v.shape[:-1] + (KC, 128)), -1, -2))


def _tile_w(W, gw):
    lead = W.shape[:-2]
    K, M = W.shape[-2:]
    Wr = W.reshape(lead + (K // 128, 128, M // gw, gw))
    n = len(lead)
    Wr = np.moveaxis(Wr, (n + 2, n + 1, n + 0, n + 3), (n + 0, n + 1, n + 2, n + 3))
    return np.ascontiguousarray(Wr).reshape(lead + (M // gw, 128, (K // 128) * gw))


def _tile_win(W):
    Ln = W.shape[0]
    G = W[:, :, :DFF].reshape(Ln, KC, 128, 22, 256)
    U = W[:, :, DFF:].reshape(Ln, KC, 128, 22, 256)
    C = np.concatenate([G, U], axis=-1)
    C = np.transpose(C, (0, 3, 2, 1, 4))
    return np.ascontiguousarray(C).reshape(Ln, 22, 128, KC * 512)


def _tables(na_rpb, s):
    ni_n = na_rpb.shape[0]
    rk = np.arange(2)[:, None, None, None, None]
    kc = np.arange(64)[None, :, None, None, None]
    e = np.arange(NE)[None, None, :, None, None]
    rq = np.arange(2)[None, None, None, :, None]
    qc = np.arange(64)[None, None, None, None, :]
    dr = 2 * (5 - e) + rk - rq
    ws = np.clip(qc - 8, 0, 48)
    colok = (kc >= ws) & (kc < ws + 16)
    shp = (2, 64, NE, 2, 64)
    dri = np.broadcast_to(np.clip(dr + 7, 0, 14), shp)
    dci = np.broadcast_to(np.clip(kc - qc + 15, 0, 30), shp)
    g = na_rpb[:, :, dri, dci]
    toes = []
    for rowok in ((dr >= -4) & (dr <= 3), np.abs(dr) <= 7):
        ok = np.broadcast_to(rowok & colok, shp)
        toes.append(np.where(ok[None, None], g, np.float32(NEG)).astype(np.float32).reshape(ni_n, NH, 128, NE * 128))
    toe = np.ascontiguousarray(np.stack(toes, axis=2))
    rv = np.zeros((2, 7 * 8, 128, 512), np.float32)
    for ni, q_tiles in enumerate((L0_Q_TILES, L2_Q_TILES)):
        for ti, (R, nr) in enumerate(q_tiles):
            fq = R + np.arange(nr)
            gq = fq - OWN0 + 32 * s
            r0 = np.clip(gq - 4, 0, 56)
            for kk, kp in enumerate(key_pairs(R, nr)):
                fk = 2 * kp + np.arange(2)
                gk = fk - OWN0 + 32 * s
                v = ((gq[None, :] >= 0) & (gq[None, :] <= 63) & (gk[:, None] >= r0[None, :]) & (gk[:, None] <= r0[None, :] + 7)
                     & (gk[:, None] >= 0) & (gk[:, None] <= 63))
                m = np.broadcast_to(v[:, None, :, None], (2, 64, nr, 64)).reshape(128, nr * 64)
                rv[ni, ti * 8 + kk, :, :nr * 64] = m
    return toe, rv.astype(ml_dtypes.bfloat16)


def _pool_masks(tg, Lseq):
    n = tg.shape[0]
    out = np.ones((5, n), np.float32)
    valid = (tg >= 0) & (tg < Lseq)
    out[0] = valid
    for g, w in enumerate((2, 4, 8, 16)):
        lo = np.clip(tg - w // 2, 0, Lseq)
        hi = np.clip(tg - w // 2 + w, 0, Lseq)
        c = np.maximum(hi - lo, 1)
        out[1 + g] = np.where(valid, 1.0 / c.astype(np.float32), 1.0)
    return out


def prep_inputs(inputs, n_cores=8, nl=DEPTH):
    f32 = np.float32
    x = np.asarray(inputs["x"], f32)
    ctx = np.asarray(inputs["ctx"], f32)
    c = np.asarray(inputs["c"], f32)
    c_ctx = np.asarray(inputs["c_ctx"], f32)
    shared = {
        "ada_w": _tile_w(np.asarray(inputs["ada_w"], f32), 512),
        "ada_b": np.ascontiguousarray(np.swapaxes(np.asarray(inputs["ada_b"], f32).reshape(DEPTH, 96, 128), 1, 2)),
        "lnp": np.ascontiguousarray(np.stack([_fm(np.asarray(inputs[k], f32)) for k in
                                              ("ln_mix_g", "ln_mix_b", "ln_ffn_g", "ln_ffn_b")], axis=2)),
        "psc": _fm(np.asarray(inputs["pool_scale"], f32)),
        "w_qkv": _tile_w(np.asarray(inputs["na_w_qkv"], f32), 512),
        "w_o": _tile_w(np.asarray(inputs["na_w_o"], f32), 512),
        "pool_w": _tile_w(np.asarray(inputs["pool_w"], f32), 512)[:, :, 0],
        "w_in": _tile_win(np.asarray(inputs["ffn_w_in"], f32)),
        "w_out": _tile_w(np.asarray(inputs["ffn_w_out"], f32), 128),
        "ident": np.eye(128, dtype=f32).astype(ml_dtypes.bfloat16),
        "pmc": np.ascontiguousarray(np.broadcast_to(_pool_masks(np.arange(CTX), CTX)[None], (128, 5, CTX))),
    }
    nna, npl = (nl + 1) // 2, max(1, nl // 2)
    for k in ("ada_w", "ada_b", "lnp", "w_in", "w_out"):
        shared[k] = shared[k][:nl]
    for k in ("w_qkv", "w_o"):
        shared[k] = shared[k][:nna]
    for k in ("psc", "pool_w"):
        shared[k] = shared[k][:npl]
    rpb = np.asarray(inputs["na_rpb"], f32)[:nna]
    maps = []
    for core in range(n_cores):
        b, s = core // 2, core % 2
        g0 = -OWN0 + 32 * s
        xf = np.zeros((T, D), f32)
        lo, hi = max(g0, 0), min(g0 + FR, 64)
        xf[(lo - g0) * GW:(hi - g0) * GW] = x[b, lo * GW:hi * GW]
        toe, rv = _tables(rpb, s)
        tg = (np.arange(T) // GW + g0) * GW + np.arange(T) % GW
        tg = np.where((np.arange(T) // GW + g0 >= 0) & (np.arange(T) // GW + g0 < 64), tg, -10 ** 6)
        m = dict(shared)
        m.update({
            "xT": np.ascontiguousarray(xf.T),
            "cxT": np.ascontiguousarray(ctx[b].T),
            "cs": np.ascontiguousarray(np.stack([_fm(c[b]), _fm(c_ctx)], axis=-1)),
            "toe": toe, "rv": rv,
            "pm": np.ascontiguousarray(np.broadcast_to(_pool_masks(tg, L)[None], (128, 5, T))),
        })
        maps.append(m)
    return maps


_NC_CACHE = {}


def kernel(**inputs):
    maps = prep_inputs(inputs)
    if "nc" not in _NC_CACHE:
        _NC_CACHE["nc"] = build_nc()
    res = run_bass_kernel_spmd(_NC_CACHE["nc"], maps, core_ids=list(range(8)))
    out = np.zeros((4, L, D), np.float32)
    for core in range(8):
        b, s = core // 2, core % 2
        out[b, s * 2048:(s + 1) * 2048] = res.results[core]["outT"].T
    return out
```

```python
import contextlib
import numpy as np
import ml_dtypes
import concourse.bass as bass
import concourse.mybir as mybir
from concourse.bass_utils import run_bass_kernel_spmd

F32 = mybir.dt.float32
BF16 = mybir.dt.bfloat16
AF = mybir.ActivationFunctionType
ALU = mybir.AluOpType

D = 2048
KC = 16
NH = 16
DFF = 5632
FC = 44
CTX = 256
GW = 64
L = 4096
FR = 50
T = FR * GW
OWN0 = 10
DEPTH = 4
ALPHA = (2 * DEPTH) ** 0.25
EPS2 = 1e-5 / ALPHA ** 2
NEG = -30000.0
NE = 11
SCALE = 128 ** -0.5

SAME_ENGINE_SYNC = True
SEM_EPOCH = 30000


class Buf:
    def __init__(self, fw, name, ap=None):
        self.name = name
        self.ap = ap
        self.lw = None
        self.rd = {}
        self.dsem = None
        self.dcnt = 0
        self.fw = fw
        self.all_writers = None

    def get_dsem(self):
        if self.dsem is None:
            self.dsem = self.fw.new_sem("d_" + self.name)
            self.fw.dma_bufs.append(self)
        return self.dsem


class Eng:
    def __init__(self, fw, name, h, is_pe=False):
        self.fw = fw
        self.name = name
        self.h = h
        self.is_pe = is_pe
        self.sem = fw.new_sem("e_" + name)
        self.cnt = 0
        self.seen = {}
        self.nsem = 1
        self.prev = None

    def bump(self):
        if self.cnt >= SEM_EPOCH:
            self.prev = (self.sem, self.cnt)
            self.sem = self.fw.new_sem("e_%s%d" % (self.name, self.nsem))
            self.nsem += 1
            self.cnt = 0


class FW:
    def __init__(self, nc):
        self.nc = nc
        self.stack = contextlib.ExitStack()
        self.dma_bufs = []
        self.nsem = 0
        self.nbuf = 0
        self.pe = Eng(self, "pe", nc.tensor, is_pe=True)
        self.act = Eng(self, "act", nc.scalar)
        self.dve = Eng(self, "dve", nc.vector)
        self.pool = Eng(self, "pool", nc.gpsimd)
        self.sp = Eng(self, "sp", nc.sync)

    def new_sem(self, name):
        self.nsem += 1
        return self.stack.enter_context(self.nc.semaphore("s%d_%s" % (self.nsem, name)))

    def sbuf(self, name, shape, dtype):
        self.nbuf += 1
        t = self.stack.enter_context(self.nc.sbuf_tensor("%s_%d" % (name, self.nbuf), list(shape), dtype))
        return Buf(self, name, t)

    def psum(self, name, shape, dtype=F32):
        self.nbuf += 1
        t = self.stack.enter_context(self.nc.psum_tensor("%s_%d" % (name, self.nbuf), list(shape), dtype))
        return Buf(self, name, t)

    def dram(self, name, ap=None):
        return Buf(self, name, ap)

    def _collect(self, reads, writes):
        waits = {}

        def need(dep):
            if dep is None:
                return
            sem, val = dep
            cur = waits.get(id(sem))
            if cur is None or cur[1] < val:
                waits[id(sem)] = (sem, val)

        for b in reads:
            need(b.lw)
        for b in writes:
            need(b.lw)
            for dep in b.rd.values():
                need(dep)
        return waits

    def _emit_waits(self, E, waits, skip_own):
        for sem, val in waits.values():
            if skip_own and sem is E.sem:
                continue
            if E.seen.get(id(sem), 0) >= val:
                continue
            E.h.wait_ge(sem, val)
            E.seen[id(sem)] = val

    def op(self, E, fn, reads=(), writes=()):
        waits = self._collect(reads, writes)
        E.bump()
        self._emit_waits(E, waits, skip_own=(E.is_pe or not SAME_ENGINE_SYNC))
        ins = fn()
        E.cnt += 1
        ins.then_inc(E.sem, 1)
        tag = (E.sem, E.cnt)
        for b in reads:
            b.rd[id(E.sem)] = tag
        for b in writes:
            b.lw = tag
            b.rd = {}
        return ins

    def dma(self, Q, out_ap, in_ap, src, dst, sembuf):
        srcs = src if isinstance(src, (list, tuple)) else [src]
        dsts = dst if isinstance(dst, (list, tuple)) else [dst]
        waits = self._collect(srcs, dsts)
        dsem = sembuf.get_dsem()
        if sembuf.dcnt > 0:
            cur = waits.get(id(dsem))
            if cur is None or cur[1] < sembuf.dcnt:
                waits[id(dsem)] = (dsem, sembuf.dcnt)
        self._emit_waits(Q, waits, skip_own=False)
        ins = Q.h.dma_start(out=out_ap, in_=in_ap)
        sembuf.dcnt += 16
        ins.then_inc(dsem, 16)
        tag = (dsem, sembuf.dcnt)
        for b in srcs:
            b.rd[id(dsem)] = tag
        for b in dsts:
            b.lw = tag
            b.rd = {}
            if b.all_writers is not None:
                b.all_writers[id(dsem)] = tag
        return ins

    def barrier(self):
        engs = [self.pe, self.act, self.dve, self.pool, self.sp]
        deps = []
        for e in engs:
            if e.prev is not None:
                deps.append(e.prev)
            if e.cnt > 0:
                deps.append((e.sem, e.cnt))
        deps += [(b.dsem, b.dcnt) for b in self.dma_bufs if b.dcnt > 0]
        for E in engs:
            for sem, val in deps:
                if sem is E.sem:
                    continue
                if E.seen.get(id(sem), 0) >= val:
                    continue
                E.h.wait_ge(sem, val)
                E.seen[id(sem)] = val

    def finish(self, E, bufs):
        waits = self._collect(bufs, [])
        for b in bufs:
            for sem, val in (b.all_writers or {}).values():
                cur = waits.get(id(sem))
                if cur is None or cur[1] < val:
                    waits[id(sem)] = (sem, val)
        self._emit_waits(E, waits, skip_own=False)

    def close(self):
        self.stack.close()


def row_tiles(r0, r1):
    tiles = []
    r = r0
    span = r1 - r0
    lead = (span % 8) // 2
    nfront = 1 if lead >= 1 else 0
    nback = lead - nfront
    for _ in range(nfront):
        tiles.append((r, 2))
        r += 2
    while r + 8 <= r1 - 2 * nback:
        tiles.append((r, 8))
        r += 8
    for _ in range(nback):
        tiles.append((r, 2))
        r += 2
    assert r == r1, (r0, r1, tiles)
    return tiles


L0_Q_TILES = [(4, 2), (6, 8), (14, 8), (22, 8), (30, 8), (38, 8), (46, 2)]
L0_KV_TILES = [(0, 8), (8, 8), (16, 8), (24, 8), (32, 8), (40, 8), (48, 2)]
L1_TILES = [(4, 2), (6, 8), (14, 8), (22, 8), (30, 8), (38, 8)]
L2_Q_TILES = [(8, 2), (10, 8), (18, 8), (26, 8), (34, 8), (42, 2)]
L3_TILES = [(10, 8), (18, 8), (26, 8), (34, 8)]
POOL_IN = {1: (4, 47), 3: (8, 44)}


def key_pairs(R, nr):
    q0 = R // 2
    hi = q0 + (5 if nr == 8 else 2)
    return [k for k in range(q0 - 2, hi + 1) if 0 <= k <= FR // 2 - 1]


def build_nc(n_layers=DEPTH):
    nc = bass.Bass("TRN2", target_bir_lowering=False)
    fw = FW(nc)

    def din(name, shape, dt=F32):
        return nc.dram_tensor(name, list(shape), dt, kind="ExternalInput").ap()

    xT_in = din("xT", [D, T])
    cxT_in = din("cxT", [D, CTX])
    cs_in = din("cs", [128, KC, 2])
    ND, NNA, NPL = n_layers, (n_layers + 1) // 2, max(1, n_layers // 2)
    ada_w = din("ada_w", [ND, 24, 128, 8192])
    ada_b = din("ada_b", [ND, 128, 96])
    lnp = din("lnp", [ND, 128, 4, KC])
    psc = din("psc", [NPL, 128, KC])
    w_qkv = din("w_qkv", [NNA, 12, 128, 8192])
    w_o = din("w_o", [NNA, 4, 128, 8192])
    pool_w = din("pool_w", [NPL, 4, 128, 2048])
    w_in = din("w_in", [ND, 22, 128, 8192])
    w_out = din("w_out", [ND, 16, 128, 5632])
    toe_in = din("toe", [NNA, NH, 2, 128, NE * 128])
    rv_in = din("rv", [2, 7 * 8, 128, 512], BF16)
    pm_in = din("pm", [128, 5, T])
    pmc_in = din("pmc", [128, 5, CTX])
    ident_in = din("ident", [128, 128], BF16)
    outT = nc.dram_tensor("outT", [D, 32 * GW], F32, kind="ExternalOutput").ap()

    qkvT = nc.dram_tensor("qkvT", [48, 128, T], BF16).ap()
    qkvcT = nc.dram_tensor("qkvcT", [48, 128, CTX], BF16).ap()
    oT = nc.dram_tensor("oT", [KC, 128, T], BF16).ap()
    ocT = nc.dram_tensor("ocT", [KC, 128, CTX], BF16).ap()
    xT = nc.dram_tensor("xTs", [D, T], F32).ap()
    cxT = nc.dram_tensor("cxTs", [D, CTX], F32).ap()

    d_const = fw.dram("const")
    d_x = fw.dram("xT")
    d_cx = fw.dram("cxT")
    d_qkv = fw.dram("qkvT")
    d_qkvc = fw.dram("qkvcT")
    d_o = fw.dram("oT")
    d_oc = fw.dram("ocT")
    d_out = fw.dram("outT")
    d_out.all_writers = {}

    xs = fw.sbuf("xs", [128, KC, 512], F32)
    hs = fw.sbuf("hs", [128, KC, 512], BF16)
    hs2 = hs
    ub = fw.sbuf("ub", [128, FC, 512], BF16)
    wbufs = [fw.sbuf("wb%d" % i, [128, 8192], BF16) for i in range(2)]
    stage = [fw.sbuf("stg%d" % i, [128, 512], BF16) for i in range(2)]
    sgb = [fw.sbuf("sg%d" % i, [128, 512], F32) for i in range(2)]
    zb = [fw.sbuf("zb%d" % i, [128, 512], BF16) for i in range(3)]
    zq = [fw.sbuf("zq%d" % i, [128, 512], BF16) for i in range(3)]
    mean_t = fw.sbuf("mean", [128, 512], F32)
    rstd_t = fw.sbuf("rstd", [128, 512], F32)
    ones_bf = fw.sbuf("ones", [128, 128], BF16)
    ident = fw.sbuf("ident", [128, 128], BF16)
    eps_t = fw.sbuf("eps", [128, 1], F32)
    cs_t = fw.sbuf("cs", [128, KC, 2], F32)
    csb = fw.sbuf("csb", [128, KC, 2], BF16)
    modvs = [fw.sbuf("modv%d" % i, [128, 96, 2], F32) for i in range(2)]
    adabs = [fw.sbuf("adab%d" % i, [128, 96], F32) for i in range(2)]
    LP = []
    for i in range(n_layers):
        LP.append({"modv": modvs[i % 2], "adab": adabs[i % 2], "lnp": fw.sbuf("lnp%d" % i, [128, 4, KC], F32),
                   "psc": fw.sbuf("psc%d" % i, [128, KC], F32),
                   "prm": [fw.sbuf("prm%d_%d" % (i, j), [128, 6, KC], F32) for j in range(2)]})
    lnp_t = LP[0]["lnp"]
    prm = LP[0]["prm"]
    NA_EL = 4 * T + 4 * CTX + 2 * NE * 128
    attA = fw.sbuf("attA", [128, NA_EL], BF16)
    pb = [fw.sbuf("pb%d" % i, [128, 512], BF16) for i in range(4)]
    rvb = [fw.sbuf("rvb%d" % i, [128, 512], BF16) for i in range(4)]
    rDs = [mean_t, rstd_t]
    pmk = fw.sbuf("pmk", [128, 5, 512], F32)

    banks = [fw.psum("bk%d" % i, [128, 512], F32) for i in range(7)]
    pst = fw.psum("pst", [128, 1024], BF16)
    mm = banks[0:3]
    st_sum, st_sq = banks[3], banks[4]
    psOs = [banks[5], banks[3]]
    psDs = [banks[6], banks[4]]

    ubf = ub.ap[:].rearrange("p a b -> p (a b)")
    xsf = xs.ap[:].rearrange("p a b -> p (a b)")
    stage4 = [Buf(fw, "st4_%d" % i, ubf[:, i * 2048:(i + 1) * 2048].rearrange("p (a b) -> p a b", a=4)) for i in range(4)]

    def mkset(name, v, toefs):
        o = [0]

        def take(n):
            r = v[:, o[0]:o[0] + n]
            o[0] += n
            return r
        S = {}
        S["qt"] = Buf(fw, name + "qt", take(T))
        S["kt"] = Buf(fw, name + "kt", take(T))
        S["vT"] = Buf(fw, name + "vT", take(T))
        S["vv"] = Buf(fw, name + "vv", take(T).rearrange("p (a b) -> p a b", b=128))
        S["kct"] = Buf(fw, name + "kct", take(CTX))
        S["vcT"] = Buf(fw, name + "vcT", take(CTX))
        S["vc"] = Buf(fw, name + "vc", take(CTX).rearrange("p (a b) -> p a b", b=128))
        S["qct"] = Buf(fw, name + "qct", take(CTX))
        S["toeb"] = Buf(fw, name + "toeb", take(2 * NE * 128).rearrange("p (a b) -> p a b", a=2))
        S["toef"] = [Buf(fw, name + "toef%d" % i, t) for i, t in enumerate(toefs)]
        return S
    wbufs.append(Buf(fw, "wb2", attA.ap[:, 0:8192]))
    nwb = [3]
    n1 = NE * 128
    setA = mkset("A", attA.ap, [xsf[:, 0:n1], xsf[:, n1:2 * n1]])
    setB = mkset("B", ubf, [xsf[:, 2 * n1:3 * n1], xsf[:, 3 * n1:4 * n1]])
    xs_c = [Buf(fw, "xs%d" % m) for m in range(KC)]
    hs_c = [Buf(fw, "hs%d" % m) for m in range(KC)]
    ub_c = [Buf(fw, "ub%d" % j) for j in range(FC)]
    pe, act, dve, pool, sp = fw.pe, fw.act, fw.dve, fw.pool, fw.sp
    cnt = {"mm": 0, "w": 0, "stg": 0, "ev": 0, "pb": 0, "z": 0, "sg": 0, "rv": 0, "st4": 0, "rd": 0}

    def nxt(key, n):
        v = cnt[key]
        cnt[key] = (v + 1) % n
        return v

    fw.op(dve, lambda: nc.vector.memset(ones_bf.ap[:], 1.0), writes=[ones_bf])
    fw.op(dve, lambda: nc.vector.memset(eps_t.ap[:], EPS2), writes=[eps_t])
    fw.dma(sp, ident.ap[:], ident_in, d_const, ident, ident)
    fw.dma(sp, cs_t.ap[:], cs_in, d_const, cs_t, cs_t)
    fw.op(act, lambda: nc.scalar.activation(out=csb.ap[:], in_=cs_t.ap[:], func=AF.Silu), reads=[cs_t], writes=[csb])

    def load_w(Wt_g, kc, tot):
        wb = wbufs[nxt("w", nwb[0])]
        fw.dma(pool, wb.ap[:, 0:kc * tot], Wt_g, d_const, wb, wb)
        view = wb.ap[:, 0:kc * tot].rearrange("p (c m) -> p c m", c=kc)
        return wb, view

    def linear_gen(h_buf, h_view, kc, N, Wt, g_list, gsz, evac):
        for gpos, g in enumerate(g_list):
            wb, wv = load_w(Wt[g], kc, gsz * 128)
            for gi in range(gsz):
                bk = mm[nxt("mm", 3)]
                for k in range(kc):
                    fw.op(pe, lambda k=k, gi=gi, bk=bk: nc.tensor.matmul(
                        bk.ap[:, 0:N], lhsT=wv[:, k, gi * 128:(gi + 1) * 128], rhs=h_view(k),
                        start=(k == 0), stop=(k == kc - 1)), reads=[wb, h_buf(k)], writes=[bk])
                evac(gpos * gsz + gi, bk, bk.ap[:, 0:N])
            yield

    def linear(*a):
        for _ in linear_gen(*a):
            pass

    def evac_copy(dst_view, src_buf, src_view, dst_buf):
        if nxt("ev", 2) == 0:
            fw.op(act, lambda: nc.scalar.copy(out=dst_view, in_=src_view), reads=[src_buf], writes=[dst_buf])
        else:
            fw.op(dve, lambda: nc.vector.tensor_copy(out=dst_view, in_=src_view), reads=[src_buf], writes=[dst_buf])

    def load_x(xT_ap, dx, t0, N):
        for c in range(KC):
            fw.dma(sp, xs.ap[:, c, 0:N], xT_ap[c * 128:(c + 1) * 128, t0:t0 + N], dx, xs_c[c], xs_c[c])

    def store_x(xT_ap, dx, t0, N):
        for c in range(KC):
            fw.dma(sp, xT_ap[c * 128:(c + 1) * 128, t0:t0 + N], xs.ap[:, c, 0:N], xs_c[c], dx, xs_c[c])

    def modulate(P, ia, ib, N, out_buf):
        for c in range(KC):
            fw.op(act, lambda c=c: nc.scalar.activation(out=hs.ap[:, c, 0:N], in_=xs.ap[:, c, 0:N], func=AF.Identity,
                                                        scale=P.ap[:, ia, c:c + 1], bias=P.ap[:, ib, c:c + 1]),
                  reads=[xs_c[c], P], writes=[hs_c[c]])

    def resid_evac(segs, iga, N):
        pend = []

        def stats(m, i):
            fw.op(pe, lambda: nc.tensor.matmul(st_sum.ap[:, 0:N], lhsT=ones_bf.ap[:], rhs=zb[i].ap[:, 0:N],
                                               start=(m == 0), stop=(m == KC - 1)), reads=[ones_bf, zb[i]], writes=[st_sum])
            fw.op(pe, lambda: nc.tensor.matmul(st_sq.ap[:, 0:N], lhsT=ones_bf.ap[:], rhs=zq[i].ap[:, 0:N],
                                               start=(m == 0), stop=(m == KC - 1)), reads=[ones_bf, zq[i]], writes=[st_sq])

        def ev(m, bk, bv):
            if pend:
                stats(*pend.pop())
            if m is None:
                return
            for (o_, n_, P) in segs:
                fw.op(dve, lambda: nc.vector.scalar_tensor_tensor(out=xs.ap[:, m, o_:o_ + n_], in0=bv[:, o_:o_ + n_],
                                                                  scalar=P.ap[:, iga, m:m + 1], in1=xs.ap[:, m, o_:o_ + n_],
                                                                  op0=ALU.mult, op1=ALU.add),
                      reads=[bk, P, xs_c[m]], writes=[xs_c[m]])
            i = nxt("z", 3)
            fw.op(act, lambda: nc.scalar.copy(out=zb[i].ap[:, 0:N], in_=xs.ap[:, m, 0:N]), reads=[xs_c[m]], writes=[zb[i]])
            fw.op(act, lambda: nc.scalar.activation(out=zq[i].ap[:, 0:N], in_=xs.ap[:, m, 0:N], func=AF.Square),
                  reads=[xs_c[m]], writes=[zq[i]])
            pend.append((m, i))
        return ev

    def ln_finish(segs, N, ig, ib, ia2=None, ib2=None):
        fw.op(act, lambda: nc.scalar.mul(out=mean_t.ap[:, 0:N], in_=st_sum.ap[:, 0:N], mul=1.0 / D),
              reads=[st_sum], writes=[mean_t])
        fw.op(dve, lambda: nc.vector.tensor_tensor(out=rstd_t.ap[:, 0:N], in0=mean_t.ap[:, 0:N], in1=mean_t.ap[:, 0:N], op=ALU.mult),
              reads=[mean_t], writes=[rstd_t])
        fw.op(dve, lambda: nc.vector.scalar_tensor_tensor(out=rstd_t.ap[:, 0:N], in0=st_sq.ap[:, 0:N], scalar=1.0 / D,
                                                          in1=rstd_t.ap[:, 0:N], op0=ALU.mult, op1=ALU.subtract),
              reads=[st_sq, rstd_t], writes=[rstd_t])
        fw.op(act, lambda: nc.scalar.activation(out=rstd_t.ap[:, 0:N], in_=rstd_t.ap[:, 0:N], func=AF.Ln, bias=eps_t.ap[:, 0:1]),
              reads=[rstd_t, eps_t], writes=[rstd_t])
        fw.op(act, lambda: nc.scalar.activation(out=rstd_t.ap[:, 0:N], in_=rstd_t.ap[:, 0:N], func=AF.Exp, scale=-0.5),
              reads=[rstd_t], writes=[rstd_t])
        def xnew(m):
            xv = xs.ap[:, m, 0:N]
            fw.op(act, lambda: nc.scalar.activation(out=xv, in_=xv, func=AF.Identity, bias=lnp_t.ap[:, ib, m:m + 1]),
                  reads=[xs_c[m], lnp_t], writes=[xs_c[m]])
        for m in range(KC):
            xv = xs.ap[:, m, 0:N]
            fw.op(dve, lambda xv=xv: nc.vector.tensor_tensor(out=xv, in0=xv, in1=mean_t.ap[:, 0:N], op=ALU.subtract),
                  reads=[xs_c[m], mean_t], writes=[xs_c[m]])
            fw.op(dve, lambda xv=xv, m=m: nc.vector.scalar_tensor_tensor(out=xv, in0=xv, scalar=lnp_t.ap[:, ig, m:m + 1],
                                                                         in1=rstd_t.ap[:, 0:N], op0=ALU.mult, op1=ALU.mult),
                  reads=[xs_c[m], lnp_t, rstd_t], writes=[xs_c[m]])
            if ia2 is not None:
                for (o_, n_, P) in segs:
                    fw.op(act, lambda: nc.scalar.activation(out=hs2.ap[:, m, o_:o_ + n_], in_=xs.ap[:, m, o_:o_ + n_],
                                                            func=AF.Identity, scale=P.ap[:, ia2, m:m + 1], bias=P.ap[:, ib2, m:m + 1]),
                          reads=[xs_c[m], P], writes=[hs_c[m]])
                if m >= 1:
                    xnew(m - 1)
            else:
                xnew(m)
        if ia2 is not None:
            xnew(KC - 1)

    def ffn(li, segs, N, mid_hook=None):
        W1 = w_in[li]
        W2 = w_out[li]
        state = {}

        def ev(i, bk, bv):
            q, r = divmod(i, 4)
            j = 2 * q + (r % 2)
            if r < 2:
                s = sgb[nxt("sg", 2)]
                fw.op(act, lambda: nc.scalar.activation(out=s.ap[:, 0:N], in_=bv, func=AF.Silu), reads=[bk], writes=[s])
                state[j] = s
            else:
                s = state.pop(j)
                fw.op(dve, lambda: nc.vector.tensor_tensor(out=ub.ap[:, j, 0:N], in0=bv, in1=s.ap[:, 0:N], op=ALU.mult),
                      reads=[bk, s], writes=[ub_c[j]])
        linear(lambda k: hs_c[k], lambda k: hs2.ap[:, k, 0:N], KC, N, W1, list(range(22)), 4, ev)
        if mid_hook is not None:
            mid_hook()
        rev = resid_evac(segs, 5, N)
        linear(lambda k: ub_c[k], lambda k: ub.ap[:, k, 0:N], FC, N, W2, list(range(KC)), 1, rev)
        rev(None, None, None)
        ln_finish(segs, N, 2, 3)

    def ada_gen(li):
        use_na_ = li % 2 == 0
        use_ctx_ = (li < 2) or use_na_
        pool_layer = not use_na_
        pj = li // 2
        lp = LP[li]
        modv, adab, lnp_l, psc_l = lp["modv"], lp["adab"], lp["lnp"], lp["psc"]
        fw.dma(sp, adab.ap[:], ada_b[li], d_const, adab, adab)
        fw.dma(sp, lnp_l.ap[:], lnp[li], d_const, lnp_l, lnp_l)
        if pool_layer:
            fw.dma(sp, psc_l.ap[:], psc[pj], d_const, psc_l, psc_l)

        def ev(i, bk, bv):
            fw.op(dve, lambda: nc.vector.tensor_scalar_add(out=modv.ap[:, i, :], in0=bk.ap[:, 0:2], scalar1=adab.ap[:, i:i + 1]),
                  reads=[bk, adab], writes=[modv])
        yield from linear_gen(lambda k: csb, lambda k: csb.ap[:, k, :], KC, 2, ada_w[li], list(range(24)), 4, ev)
        for j in range(2 if use_ctx_ else 1):
            P = lp["prm"][j]
            sh1, sc1, g1 = modv.ap[:, 0:16, j], modv.ap[:, 16:32, j], modv.ap[:, 32:48, j]
            sh2, sc2, g2 = modv.ap[:, 48:64, j], modv.ap[:, 64:80, j], modv.ap[:, 80:96, j]
            o = lambda fn: fw.op(dve, fn, reads=[modv, lnp_l, psc_l, P], writes=[P])
            o(lambda: nc.vector.tensor_scalar_add(out=P.ap[:, 0, :], in0=sc1, scalar1=1.0))
            o(lambda: nc.vector.tensor_copy(out=P.ap[:, 1, :], in_=sh1))
            o(lambda: nc.vector.tensor_scalar_mul(out=P.ap[:, 2, :], in0=g1, scalar1=1.0 / ALPHA))
            if pool_layer:
                o(lambda: nc.vector.tensor_tensor(out=P.ap[:, 2, :], in0=P.ap[:, 2, :], in1=psc_l.ap[:], op=ALU.mult))
            o(lambda: nc.vector.tensor_scalar_add(out=P.ap[:, 3, :], in0=sc2, scalar1=1.0))
            o(lambda: nc.vector.tensor_tensor(out=P.ap[:, 4, :], in0=P.ap[:, 3, :], in1=lnp_l.ap[:, 1, :], op=ALU.mult))
            o(lambda: nc.vector.tensor_tensor(out=P.ap[:, 4, :], in0=P.ap[:, 4, :], in1=sh2, op=ALU.add))
            o(lambda: nc.vector.tensor_scalar_mul(out=P.ap[:, 5, :], in0=g2, scalar1=1.0 / ALPHA))
        yield

    def chain(gens):
        for g in gens:
            yield from g

    def pull(bg, n):
        if bg is None:
            return
        for _ in range(n):
            try:
                next(bg)
            except StopIteration:
                return

    def qkv_phase(ni, xT_ap, dx, tiles_tok, P, dst_ap, ddst, chunk_list):
        Wq = w_qkv[ni]
        for (t0, N) in tiles_tok:
            load_x(xT_ap, dx, t0, N)
            modulate(P, 0, 1, N, hs)
            cur = {}

            def ev(i, bk, bv, t0=t0, N=N):
                gi = i % 4
                if gi == 0:
                    cur["b"] = stage4[nxt("st4", 4)]
                sb_ = cur["b"]
                evac_copy(sb_.ap[:, gi, 0:N], bk, bv, sb_)
                if gi == 3:
                    c0 = chunk_list[i - 3]
                    fw.dma(sp, dst_ap[c0:c0 + 4, :, t0:t0 + N].rearrange("c p t -> p c t"), sb_.ap[:, :, 0:N], sb_, ddst, sb_)
            linear(lambda k: hs_c[k], lambda k, N=N: hs.ap[:, k, 0:N], KC, N, Wq, [c // 4 for c in chunk_list[::4]], 4, ev)

    def transpose_v(src, dst, npairs):
        for p0 in range(0, npairs, 8):
            n = min(8, npairs - p0)
            for i in range(n):
                fw.op(pe, lambda i=i, p0=p0: nc.tensor.transpose(out=pst.ap[:, i * 128:(i + 1) * 128],
                                                                 in_=src.ap[:, (p0 + i) * 128:(p0 + i + 1) * 128],
                                                                 identity=ident.ap[:]), reads=[src, ident], writes=[pst])
            fw.op(dve, lambda p0=p0, n=n: nc.vector.tensor_copy(
                out=dst.ap[:, p0:p0 + n, :], in_=pst.ap[:, 0:n * 128].rearrange("p (a b) -> p a b", b=128)),
                reads=[pst], writes=[dst])

    LA = 2
    att = {"o": 0}

    def attend_tiles(tiles, tile_hook=None):
        steps = []
        for t in tiles:
            oi = att["o"]
            att["o"] = 1 - oi
            t["pO"], t["pD"] = psOs[oi], psDs[oi]
            nk = len(t["keys"])
            for ki in range(nk):
                steps.append((t, ki, nk))
        pbufs = {}
        for i in range(len(steps) + LA):
            if i < len(steps):
                t, ki, nk = steps[i]
                Nq, qbuf, qview = t["Nq"], t["qbuf"], t["qview"]
                kv_, kb_, vv_, vb_, toe_v, toe_b, rv_v, rv_b = t["keys"][ki]
                bk = mm[nxt("mm", 3)]
                fw.op(pe, lambda: nc.tensor.matmul(bk.ap[:, 0:Nq], lhsT=kv_, rhs=qview, start=True, stop=True),
                      reads=[kb_, qbuf], writes=[bk])
                p_ = pb[nxt("pb", 4)]
                fw.op(act, lambda: nc.scalar.activation(out=p_.ap[:, 0:Nq], in_=bk.ap[:, 0:Nq], func=AF.Exp, scale=SCALE),
                      reads=[bk], writes=[p_])
                if toe_v is not None:
                    fw.op(dve, lambda: nc.vector.tensor_tensor(out=p_.ap[:, 0:Nq], in0=p_.ap[:, 0:Nq], in1=toe_v, op=ALU.mult),
                          reads=[p_, toe_b], writes=[p_])
                if rv_v is not None:
                    fw.op(pool, lambda: nc.gpsimd.tensor_tensor(out=p_.ap[:, 0:Nq], in0=p_.ap[:, 0:Nq], in1=rv_v, op=ALU.mult),
                          reads=[p_, rv_b], writes=[p_])
                pbufs[i] = p_
            j = i - LA
            if j >= 0:
                t, ki, nk = steps[j]
                Nq, pO, pD = t["Nq"], t["pO"], t["pD"]
                vv_, vb_ = t["keys"][ki][2], t["keys"][ki][3]
                p_ = pbufs.pop(j)
                fw.op(pe, lambda: nc.tensor.matmul(pO.ap[:, 0:Nq], lhsT=vv_, rhs=p_.ap[:, 0:Nq], start=(ki == 0), stop=(ki == nk - 1)),
                      reads=[vb_, p_], writes=[pO])
                fw.op(pe, lambda: nc.tensor.matmul(pD.ap[:, 0:Nq], lhsT=ones_bf.ap[:], rhs=p_.ap[:, 0:Nq], start=(ki == 0),
                                                   stop=(ki == nk - 1)), reads=[ones_bf, p_], writes=[pD])
                if ki == nk - 1:
                    rD_ = rDs[nxt("rd", 2)]
                    fw.op(act, lambda: nc.scalar.activation(out=rD_.ap[:, 0:Nq], in_=pD.ap[:, 0:Nq], func=AF.Ln), reads=[pD], writes=[rD_])
                    fw.op(act, lambda: nc.scalar.activation(out=rD_.ap[:, 0:Nq], in_=rD_.ap[:, 0:Nq], func=AF.Exp, scale=-1.0),
                          reads=[rD_], writes=[rD_])
                    sg_ = stage[nxt("stg", 2)]
                    fw.op(dve, lambda: nc.vector.tensor_tensor(out=sg_.ap[:, 0:Nq], in0=pO.ap[:, 0:Nq], in1=rD_.ap[:, 0:Nq], op=ALU.mult),
                          reads=[pO, rD_], writes=[sg_])
                    fw.dma(sp, t["out_ap"], sg_.ap[:, 0:Nq], sg_, t["dout"], sg_)
                    if tile_hook is not None:
                        tile_hook()

    def attention_phase(ni, q_tiles, with_ctx_q, edge_tiles, bg, n_bg):
        sets = [setA, setB]

        def load_dmas(h, S):
            fw.dma(sp, S["qt"].ap, qkvT[h], d_qkv, S["qt"], S["qt"])
            fw.dma(sp, S["kt"].ap, qkvT[16 + h], d_qkv, S["kt"], S["kt"])
            fw.dma(sp, S["vT"].ap, qkvT[32 + h], d_qkv, S["vT"], S["vT"])
            fw.dma(sp, S["kct"].ap, qkvcT[16 + h], d_qkvc, S["kct"], S["kct"])
            fw.dma(sp, S["vcT"].ap, qkvcT[32 + h], d_qkvc, S["vcT"], S["vcT"])
            if with_ctx_q:
                fw.dma(sp, S["qct"].ap, qkvcT[h], d_qkvc, S["qct"], S["qct"])
            for w_ in range(2):
                fw.dma(sp, S["toef"][w_].ap, toe_in[ni, h, w_], d_const, S["toef"][w_], S["toef"][w_])

        def late(S):
            for w_ in range(2):
                fw.op(act, lambda w_=w_: nc.scalar.activation(out=S["toeb"].ap[:, w_, :], in_=S["toef"][w_].ap, func=AF.Exp),
                      reads=[S["toef"][w_]], writes=[S["toeb"]])
            transpose_v(S["vT"], S["vv"], FR // 2)
            transpose_v(S["vcT"], S["vc"], 2)

        hsf = hs.ap[:].rearrange("p a b -> p (a b)")
        rvres = Buf(fw, "rvres", hsf.rearrange("p (k t) -> p k t", t=512))
        e_idx = {}
        for ei, et in enumerate(edge_tiles):
            ti = q_tiles.index(et)
            e_idx[et] = ei
            fw.dma(sp, rvres.ap[:, ei * 8:(ei + 1) * 8, :], rv_in[ni, ti * 8:(ti + 1) * 8].rearrange("k p t -> p k t"),
                   d_const, rvres, rvres)
        tcount = [0]

        def tile_hook():
            tcount[0] += 1
            if tcount[0] % n_bg == 0:
                pull(bg, 1)
        load_dmas(0, sets[0])
        for h in range(NH):
            S = sets[h % 2]
            if h + 1 < NH:
                load_dmas(h + 1, sets[(h + 1) % 2])
            late(S)
            qt_, kt_, vv_, kct_, vc_, toeb_ = S["qt"], S["kt"], S["vv"], S["kct"], S["vc"], S["toeb"]
            ckeys = [(kct_.ap[:, i * 128:(i + 1) * 128], kct_, vc_.ap[:, i, :], vc_, None, None, None, None) for i in range(2)]
            tiles = []
            for ti, (R, nr) in enumerate(q_tiles):
                Nq = nr * GW
                q0 = R // 2
                edge = (R, nr) in edge_tiles
                keys = []
                for kk, kp in enumerate(key_pairs(R, nr)):
                    e0 = 5 - (kp - q0)
                    keys.append((kt_.ap[:, kp * 128:(kp + 1) * 128], kt_, vv_.ap[:, kp, :], vv_,
                                 toeb_.ap[:, 1 if edge else 0, e0 * 128:e0 * 128 + Nq], toeb_,
                                 rvres.ap[:, e_idx[(R, nr)] * 8 + kk, 0:Nq] if edge else None, rvres))
                tiles.append(dict(qbuf=qt_, qview=qt_.ap[:, R * GW:R * GW + Nq], Nq=Nq, keys=keys + ckeys,
                                  out_ap=oT[h, :, R * GW:R * GW + Nq], dout=d_o))
            if with_ctx_q:
                tiles.append(dict(qbuf=S["qct"], qview=S["qct"].ap, Nq=CTX, keys=ckeys, out_ap=ocT[h], dout=d_oc))
            attend_tiles(tiles, tile_hook)

    def mixer_out_phase(li, tiles, wfn):
        def offs(tile):
            o, r = 0, []
            for sg in tile:
                r.append((o, sg))
                o += sg["n"]
            return r, o

        def load_mix(tile, Q):
            for o, sg in offs(tile)[0]:
                for c in range(KC):
                    fw.dma(Q, hs.ap[:, c, o:o + sg["n"]], sg["src"][c, :, sg["t0"]:sg["t0"] + sg["n"]], sg["dsrc"], hs_c[c], hs_c[c])
        for i, tile in enumerate(tiles):
            so, N = offs(tile)
            segs = [(o, sg["n"], sg["P"]) for o, sg in so]
            if i == 0:
                load_mix(tile, sp)
            for o, sg in so:
                for c in range(KC):
                    fw.dma(sp, xs.ap[:, c, o:o + sg["n"]], sg["xl"][c * 128:(c + 1) * 128, sg["t0"]:sg["t0"] + sg["n"]],
                           sg["dxl"], xs_c[c], xs_c[c])
            wfn(N, resid_evac(segs, 2, N))
            ln_finish(segs, N, 0, 1, 3, 4)
            hook = None
            if i + 1 < len(tiles):
                hook = lambda nt=tiles[i + 1]: load_mix(nt, act)
            ffn(li, segs, N, hook)
            for o, sg in so:
                for c in range(KC):
                    fw.dma(sp, sg["xst"][c * 128:(c + 1) * 128, sg["t0"]:sg["t0"] + sg["n"]], xs.ap[:, c, o:o + sg["n"]],
                           xs_c[c], sg["dxst"], xs_c[c])
                if sg["fo"] is not None:
                    sg["fo"](sg["t0"], sg["n"], o)

    def mk_tiles(lat_tiles, lat, ctx):
        full = [[dict(lat, t0=t0, n=n)] for (t0, n) in lat_tiles if n == 512]
        merged = [dict(lat, t0=t0, n=n) for (t0, n) in lat_tiles if n != 512]
        if ctx is not None:
            merged.append(dict(ctx, t0=0, n=CTX))
        assert sum(sg["n"] for sg in merged) <= 512
        return ([merged] if merged else []) + full

    def tok(tiles):
        return [(R * GW, nr * GW) for (R, nr) in tiles]

    def na_layer(li, ni, kv_tiles, q_tiles, update_ctx, xsrc, cxsrc, dxl, dcl, edge_tiles, bg, n_bg):
        Pl, Pc = prm[0], prm[1]
        allc = list(range(48))
        fw.barrier()
        qkv_phase(ni, xsrc, dxl, tok(kv_tiles), Pl, qkvT, d_qkv, allc)
        qkv_phase(ni, cxsrc, dcl, [(0, CTX)], Pc, qkvcT, d_qkvc, allc if update_ctx else list(range(16, 48)))
        fw.barrier()
        nwb[0] = 2
        cnt["w"] = 0
        attention_phase(ni, q_tiles, update_ctx, edge_tiles, bg, n_bg)
        pull(bg, 10 ** 6)
        fw.barrier()
        nwb[0] = 3
        Wo = w_o[ni]

        def wfn(N, ev):
            linear(lambda k: hs_c[k], lambda k: hs.ap[:, k, 0:N], KC, N, Wo, list(range(4)), 4, ev)
            ev(None, None, None)
        return wfn

    def pool_phase(xT_ap, dx, pm_ap, t_lo, t_hi, o_lo, o_hi, P, dst_ap, ddst, bg=None):
        for s0 in range(o_lo, o_hi, 496):
            N = min(496, o_hi - s0)
            a = max(t_lo, s0 - 8)
            b = min(t_hi, s0 + N + 8)
            off = a - (s0 - 8)
            W_ = N + 16
            n_in = b - a
            fw.dma(sp, pmk.ap[:, :, off:off + n_in], pm_ap[:, :, a:b], d_const, pmk, pmk)
            def prefetch(c):
                b0 = (c % 2) * 8
                hm = xs.ap[:, b0 + 1, :]
                xin = xs.ap[:, b0 + 4, :]
                eh = nc.vector if c % 2 == 0 else nc.gpsimd
                fw.op(dve if c % 2 == 0 else pool, lambda: eh.memset(hm, 0.0), writes=[xs_c[b0 + 1]])
                fw.dma(sp, xin[:, off:off + n_in], xT_ap[c * 128:(c + 1) * 128, a:b], dx, xs_c[b0 + 4], xs_c[b0 + 4])
            prefetch(0)
            for c in range(KC):
                g = c // 4
                b0 = (c % 2) * 8
                hA, hm, s1, s2, xin, hAw = [xs.ap[:, b0 + i, :] for i in range(6)]
                tA, tm, t1, t2, tin, tw = [xs_c[b0 + i] for i in range(6)]
                eng = dve if c % 2 == 0 else pool
                eh = nc.vector if c % 2 == 0 else nc.gpsimd
                if c + 1 < KC:
                    prefetch(c + 1)
                fw.op(act, lambda: nc.scalar.activation(out=hA[:, off:off + n_in], in_=xin[:, off:off + n_in], func=AF.Identity,
                                                        scale=P.ap[:, 0, c:c + 1], bias=P.ap[:, 1, c:c + 1]),
                      reads=[tin, P], writes=[tA])
                fw.op(eng, lambda: eh.tensor_tensor(out=hm[:, off:off + n_in], in0=hA[:, off:off + n_in],
                                                    in1=pmk.ap[:, 0, off:off + n_in], op=ALU.mult), reads=[tA, pmk], writes=[tm])
                fw.op(eng, lambda: eh.tensor_tensor(out=s1[:, 1:W_], in0=hm[:, 0:W_ - 1], in1=hm[:, 1:W_], op=ALU.add),
                      reads=[tm], writes=[t1])
                cur, other, tcur, tother = s1, s2, t1, t2
                sh = 1
                for _ in range(g):
                    fw.op(eng, lambda cur=cur, other=other, sh=sh: eh.tensor_tensor(
                        out=other[:, sh:W_ - sh], in0=cur[:, 0:W_ - 2 * sh], in1=cur[:, 2 * sh:W_], op=ALU.add),
                        reads=[tcur], writes=[tother])
                    cur, other, tcur, tother = other, cur, tother, tcur
                    sh *= 2
                fw.op(eng, lambda cur=cur: eh.tensor_tensor(out=hAw[:, 8:8 + N], in0=cur[:, 8:8 + N],
                                                            in1=pmk.ap[:, 1 + g, 8:8 + N], op=ALU.mult), reads=[tcur, pmk], writes=[tw])
                sg_ = stage[nxt("stg", 2)]
                fw.op(eng, lambda: eh.tensor_tensor(out=sg_.ap[:, 0:N], in0=hAw[:, 8:8 + N], in1=hA[:, 8:8 + N], op=ALU.subtract),
                      reads=[tw, tA], writes=[sg_])
                fw.dma(sp, dst_ap[c, :, s0:s0 + N], sg_.ap[:, 0:N], sg_, ddst, sg_)
                if c % 3 == 2:
                    pull(bg, 1)
        pull(bg, 10 ** 6)

    def pool_wfn(pj):
        def wfn(N, ev):
            for g in range(4):
                linear(lambda k, g=g: hs_c[4 * g + k], lambda k, g=g: hs.ap[:, 4 * g + k, 0:N], 4, N, pool_w[pj], [g], 4,
                       lambda i, bk, bv, g=g: ev(4 * g + i, bk, bv))
            ev(None, None, None)
        return wfn

    last_na = 2
    final_bufs = [d_out]
    for li in range(n_layers):
        use_na = li % 2 == 0
        jj = li // 2
        update_ctx = li < last_na
        use_ctx = update_ctx or use_na
        xsrc = xT_in if li == 0 else xT
        cxsrc = cxT_in if li == 0 else cxT
        if li == 0:
            pull(ada_gen(0), 10 ** 6)
        lnp_t = LP[li]["lnp"]
        prm = LP[li]["prm"]
        is_last = li == n_layers - 1

        def final_out(t0, N, o):
            lo = max(t0, OWN0 * GW)
            hi = min(t0 + N, (OWN0 + 32) * GW)
            if hi > lo:
                for c in range(KC):
                    fw.dma(sp, outT[c * 128:(c + 1) * 128, lo - OWN0 * GW:hi - OWN0 * GW],
                           xs.ap[:, c, o + lo - t0:o + hi - t0], xs_c[c], d_out, xs_c[c])

        fo = final_out if is_last else None
        dxl = d_const if li == 0 else d_x
        dcl = d_const if li == 0 else d_cx
        if use_na:
            kv_tiles, q_tiles = (L0_KV_TILES, L0_Q_TILES) if li == 0 else (L1_TILES, L2_Q_TILES)
            edge_tiles = [(6, 8), (38, 8)] if li == 0 else [(10, 8), (34, 8)]
            nxt_layers = [l for l in ((1,) if li == 0 else (3,)) if l < n_layers]
            bg = chain([ada_gen(l) for l in nxt_layers])
            wfn = na_layer(li, jj, kv_tiles, q_tiles, update_ctx, xsrc, cxsrc, dxl, dcl, edge_tiles, bg, 4 if li == 0 else 3)
            lat = dict(src=oT, dsrc=d_o, xl=xsrc, dxl=dxl, xst=xT, dxst=d_x, P=prm[0], fo=fo)
            ctxs = dict(src=ocT, dsrc=d_oc, xl=cxsrc, dxl=dcl, xst=cxT, dxst=d_cx, P=prm[1], fo=None) if update_ctx else None
            mixer_out_phase(li, mk_tiles(tok(q_tiles), lat, ctxs), wfn)
        else:
            tiles = L1_TILES if li == 1 else L3_TILES
            a, b = POOL_IN[li]
            o_lo, o_hi = tiles[0][0] * GW, (tiles[-1][0] + tiles[-1][1]) * GW
            bgp = ada_gen(2) if (li == 1 and n_layers > 2) else None
            pool_phase(xT, d_x, pm_in, a * GW, b * GW, o_lo, o_hi, prm[0], oT, d_o, bgp)
            if update_ctx:
                pool_phase(cxT, d_cx, pmc_in, 0, CTX, 0, CTX, prm[1], ocT, d_oc)
            lat = dict(src=oT, dsrc=d_o, xl=xT, dxl=d_x, xst=xT, dxst=d_x, P=prm[0], fo=fo)
            ctxs = dict(src=ocT, dsrc=d_oc, xl=cxT, dxl=d_cx, xst=cxT, dxst=d_cx, P=prm[1], fo=None) if update_ctx else None
            mixer_out_phase(li, mk_tiles(tok(tiles), lat, ctxs), pool_wfn(jj))
    fw.finish(sp, final_bufs)
    fw.close()
    return nc


def _fm(v):
    return np.ascontiguousarray(np.swapaxes(v.reshape(v.shape[:-1] + (KC, 128)), -1, -2))


def _tile_w(W, gw):
    lead = W.shape[:-2]
    K, M = W.shape[-2:]
    Wr = W.reshape(lead + (K // 128, 128, M // gw, gw))
    n = len(lead)
    Wr = np.moveaxis(Wr, (n + 2, n + 1, n + 0, n + 3), (n + 0, n + 1, n + 2, n + 3))
    return np.ascontiguousarray(Wr).reshape(lead + (M // gw, 128, (K // 128) * gw))


def _tile_win(W):
    Ln = W.shape[0]
    G = W[:, :, :DFF].reshape(Ln, KC, 128, 22, 256)
    U = W[:, :, DFF:].reshape(Ln, KC, 128, 22, 256)
    C = np.concatenate([G, U], axis=-1)
    C = np.transpose(C, (0, 3, 2, 1, 4))
    return np.ascontiguousarray(C).reshape(Ln, 22, 128, KC * 512)


def _tables(na_rpb, s):
    ni_n = na_rpb.shape[0]
    rk = np.arange(2)[:, None, None, None, None]
    kc = np.arange(64)[None, :, None, None, None]
    e = np.arange(NE)[None, None, :, None, None]
    rq = np.arange(2)[None, None, None, :, None]
    qc = np.arange(64)[None, None, None, None, :]
    dr = 2 * (5 - e) + rk - rq
    ws = np.clip(qc - 8, 0, 48)
    colok = (kc >= ws) & (kc < ws + 16)
    shp = (2, 64, NE, 2, 64)
    dri = np.broadcast_to(np.clip(dr + 7, 0, 14), shp)
    dci = np.broadcast_to(np.clip(kc - qc + 15, 0, 30), shp)
    g = na_rpb[:, :, dri, dci]
    toes = []
    for rowok in ((dr >= -4) & (dr <= 3), np.abs(dr) <= 7):
        ok = np.broadcast_to(rowok & colok, shp)
        toes.append(np.where(ok[None, None], g, np.float32(NEG)).astype(np.float32).reshape(ni_n, NH, 128, NE * 128))
    toe = np.ascontiguousarray(np.stack(toes, axis=2))
    rv = np.zeros((2, 7 * 8, 128, 512), np.float32)
    for ni, q_tiles in enumerate((L0_Q_TILES, L2_Q_TILES)):
        for ti, (R, nr) in enumerate(q_tiles):
            fq = R + np.arange(nr)
            gq = fq - OWN0 + 32 * s
            r0 = np.clip(gq - 4, 0, 56)
            for kk, kp in enumerate(key_pairs(R, nr)):
                fk = 2 * kp + np.arange(2)
                gk = fk - OWN0 + 32 * s
                v = ((gq[None, :] >= 0) & (gq[None, :] <= 63) & (gk[:, None] >= r0[None, :]) & (gk[:, None] <= r0[None, :] + 7)
                     & (gk[:, None] >= 0) & (gk[:, None] <= 63))
                m = np.broadcast_to(v[:, None, :, None], (2, 64, nr, 64)).reshape(128, nr * 64)
                rv[ni, ti * 8 + kk, :, :nr * 64] = m
    return toe, rv.astype(ml_dtypes.bfloat16)


def _pool_masks(tg, Lseq):
    n = tg.shape[0]
    out = np.ones((5, n), np.float32)
    valid = (tg >= 0) & (tg < Lseq)
    out[0] = valid
    for g, w in enumerate((2, 4, 8, 16)):
        lo = np.clip(tg - w // 2, 0, Lseq)
        hi = np.clip(tg - w // 2 + w, 0, Lseq)
        c = np.maximum(hi - lo, 1)
        out[1 + g] = np.where(valid, 1.0 / c.astype(np.float32), 1.0)
    return out


def prep_inputs(inputs, n_cores=8, nl=DEPTH):
    f32 = np.float32
    x = np.asarray(inputs["x"], f32)
    ctx = np.asarray(inputs["ctx"], f32)
    c = np.asarray(inputs["c"], f32)
    c_ctx = np.asarray(inputs["c_ctx"], f32)
    shared = {
        "ada_w": _tile_w(np.asarray(inputs["ada_w"], f32), 512),
        "ada_b": np.ascontiguousarray(np.swapaxes(np.asarray(inputs["ada_b"], f32).reshape(DEPTH, 96, 128), 1, 2)),
        "lnp": np.ascontiguousarray(np.stack([_fm(np.asarray(inputs[k], f32)) for k in
                                              ("ln_mix_g", "ln_mix_b", "ln_ffn_g", "ln_ffn_b")], axis=2)),
        "psc": _fm(np.asarray(inputs["pool_scale"], f32)),
        "w_qkv": _tile_w(np.asarray(inputs["na_w_qkv"], f32), 512),
        "w_o": _tile_w(np.asarray(inputs["na_w_o"], f32), 512),
        "pool_w": _tile_w(np.asarray(inputs["pool_w"], f32), 512)[:, :, 0],
        "w_in": _tile_win(np.asarray(inputs["ffn_w_in"], f32)),
        "w_out": _tile_w(np.asarray(inputs["ffn_w_out"], f32), 128),
        "ident": np.eye(128, dtype=f32).astype(ml_dtypes.bfloat16),
        "pmc": np.ascontiguousarray(np.broadcast_to(_pool_masks(np.arange(CTX), CTX)[None], (128, 5, CTX))),
    }
    nna, npl = (nl + 1) // 2, max(1, nl // 2)
    for k in ("ada_w", "ada_b", "lnp", "w_in", "w_out"):
        shared[k] = shared[k][:nl]
    for k in ("w_qkv", "w_o"):
        shared[k] = shared[k][:nna]
    for k in ("psc", "pool_w"):
        shared[k] = shared[k][:npl]
    rpb = np.asarray(inputs["na_rpb"], f32)[:nna]
    maps = []
    for core in range(n_cores):
        b, s = core // 2, core % 2
        g0 = -OWN0 + 32 * s
        xf = np.zeros((T, D), f32)
        lo, hi = max(g0, 0), min(g0 + FR, 64)
        xf[(lo - g0) * GW:(hi - g0) * GW] = x[b, lo * GW:hi * GW]
        toe, rv = _tables(rpb, s)
        tg = (np.arange(T) // GW + g0) * GW + np.arange(T) % GW
        tg = np.where((np.arange(T) // GW + g0 >= 0) & (np.arange(T) // GW + g0 < 64), tg, -10 ** 6)
        m = dict(shared)
        m.update({
            "xT": np.ascontiguousarray(xf.T),
            "cxT": np.ascontiguousarray(ctx[b].T),
            "cs": np.ascontiguousarray(np.stack([_fm(c[b]), _fm(c_ctx)], axis=-1)),
            "toe": toe, "rv": rv,
            "pm": np.ascontiguousarray(np.broadcast_to(_pool_masks(tg, L)[None], (128, 5, T))),
        })
        maps.append(m)
    return maps


_NC_CACHE = {}


def kernel(**inputs):
    maps = prep_inputs(inputs)
    if "nc" not in _NC_CACHE:
        _NC_CACHE["nc"] = build_nc()
    res = run_bass_kernel_spmd(_NC_CACHE["nc"], maps, core_ids=list(range(8)))
    out = np.zeros((4, L, D), np.float32)
    for core in range(8):
        b, s = core // 2, core % 2
        out[b, s * 2048:(s + 1) * 2048] = res.results[core]["outT"].T
    return out
```
